# Optimizing a Trainium2 kernel written in Bass

```python
import math
import jax, jax.numpy as jnp
from jax import lax
import numpy as np


D_MODEL = 2048
BATCH = 4
SEQ = 4096
DEPTH = 2

PLE_DIM = 256
N_BRANCH = 3
BRANCH_WIDTH = D_MODEL // 2
A_HEADS = 8
A_HEAD_DIM = BRANCH_WIDTH // A_HEADS
MOBA_BLOCK = 256
MOBA_TOPK = 3
MOBA_CHUNK = 32
B_Q_HEADS = 16
B_KV_HEADS = 4
B_HEAD_DIM = BRANCH_WIDTH // B_Q_HEADS
B_KV_WIDTH = B_KV_HEADS * B_HEAD_DIM
WINDOW = 128
C_BLOCKS = 8
C_BLOCK_DIM = BRANCH_WIDTH // C_BLOCKS
CONV_WIDTH = 4
LRU_C = 8.0
RPE_BUCKETS = 32
RPE_MAX_DIST = 128
RPE_HEADS = A_HEADS + B_Q_HEADS
EPS = 1e-6
NEG = -1e30
IN_SPLIT_SIZES = (BRANCH_WIDTH, BRANCH_WIDTH, BRANCH_WIDTH, BRANCH_WIDTH,
                  BRANCH_WIDTH, B_KV_WIDTH, B_KV_WIDTH, BRANCH_WIDTH,
                  BRANCH_WIDTH, BRANCH_WIDTH,
                  N_BRANCH * D_MODEL)
IN_WIDTH = sum(IN_SPLIT_SIZES)

kernel_name = 'hybrid_moba_swa_rglru_gated_merge'


def rmsnorm(x, g):
    xf = x.astype(jnp.float32)
    y = xf * lax.rsqrt(jnp.mean(xf * xf, axis=-1, keepdims=True) + EPS)
    return (y * g.astype(jnp.float32)).astype(x.dtype)


def t5_bucket(dist):
    n = jnp.maximum(dist, 0)
    max_exact = RPE_BUCKETS // 2
    large = max_exact + (jnp.log(jnp.maximum(n, 1).astype(jnp.float32) / max_exact)
                         / math.log(RPE_MAX_DIST / max_exact)
                         * (RPE_BUCKETS - max_exact)).astype(jnp.int32)
    large = jnp.minimum(large, RPE_BUCKETS - 1)
    return jnp.where(n < max_exact, n, large)


def moba_attention(q, k, v, table):
    bsz, seq, nh, dh = q.shape
    L = MOBA_BLOCK
    nb = -(-seq // L)
    pad = nb * L - seq
    scale = dh ** -0.5
    qh = q.transpose(0, 2, 1, 3)
    kb = jnp.pad(k, ((0, 0), (0, pad), (0, 0), (0, 0))).reshape(bsz, nb, L, nh, dh).transpose(0, 3, 1, 2, 4)
    vb = jnp.pad(v, ((0, 0), (0, pad), (0, 0), (0, 0))).reshape(bsz, nb, L, nh, dh).transpose(0, 3, 1, 2, 4)
    qblk = jnp.arange(seq) // L
    n_sel = min(MOBA_TOPK, nb - 1)
    if n_sel > 0:
        kmean = jnp.mean(kb.astype(jnp.float32), axis=3)
        gate = jnp.einsum('bhsd,bhnd->bhsn', qh.astype(jnp.float32), kmean)
        past = jnp.arange(nb)[None, :] < qblk[:, None]
        gate = jnp.where(past, gate, NEG)
        _, sel = lax.top_k(gate, n_sel)
    bi = jnp.arange(bsz)[:, None, None, None]
    hi = jnp.arange(nh)[None, :, None, None]

    def chunk(c):
        start = c * MOBA_CHUNK
        qc = lax.dynamic_slice_in_dim(qh, start, MOBA_CHUNK, axis=2)
        qpos = start + jnp.arange(MOBA_CHUNK)
        j = start // L
        k_own = lax.dynamic_index_in_dim(kb, j, axis=2, keepdims=False)
        v_own = lax.dynamic_index_in_dim(vb, j, axis=2, keepdims=False)
        dist = qpos[:, None] - (j * L + jnp.arange(L))[None, :]
        s_own = jnp.einsum('bhcd,bhld->bhcl', qc, k_own, preferred_element_type=jnp.float32) * scale
        s_own = s_own + table[:, t5_bucket(dist)].astype(jnp.float32)
        s_own = jnp.where(dist >= 0, s_own, NEG)
        if n_sel == 0:
            p_own = jax.nn.softmax(s_own, axis=-1).astype(v.dtype)
            return jnp.einsum('bhcl,bhld->bhcd', p_own, v_own)
        idx = lax.dynamic_slice_in_dim(sel, start, MOBA_CHUNK, axis=2)
        k_g = kb[bi, hi, idx]
        v_g = vb[bi, hi, idx]
        dist_g = qpos[None, None, :, None, None] - (idx[..., None] * L + jnp.arange(L))
        s_g = jnp.einsum('bhcd,bhcnld->bhcnl', qc, k_g, preferred_element_type=jnp.float32) * scale
        s_g = s_g + table[hi[..., None], t5_bucket(dist_g)].astype(jnp.float32)
        s_g = jnp.where((idx < j)[..., None], s_g, NEG)
        logits = jnp.concatenate([s_g.reshape(bsz, nh, MOBA_CHUNK, n_sel * L), s_own], axis=-1)
        p = jax.nn.softmax(logits, axis=-1).astype(v.dtype)
        p_g = p[..., :n_sel * L].reshape(bsz, nh, MOBA_CHUNK, n_sel, L)
        p_own = p[..., n_sel * L:]
        return (jnp.einsum('bhcnl,bhcnld->bhcd', p_g, v_g)
                + jnp.einsum('bhcl,bhld->bhcd', p_own, v_own))

    out = lax.map(chunk, jnp.arange(seq // MOBA_CHUNK))
    return out.transpose(1, 0, 3, 2, 4).reshape(bsz, seq, nh * dh)


def swa_attention(q, k, v, sinks, table):
    bsz, seq, hq, dh = q.shape
    hkv = k.shape[2]
    grp = hq // hkv
    W = WINDOW
    nb = seq // W
    scale = dh ** -0.5
    qb = q.reshape(bsz, nb, W, hkv, grp, dh)
    kb = k.reshape(bsz, nb, W, hkv, dh)
    vb = v.reshape(bsz, nb, W, hkv, dh)
    kband = jnp.concatenate([jnp.pad(kb, ((0, 0), (1, 0), (0, 0), (0, 0), (0, 0)))[:, :-1], kb], axis=2)
    vband = jnp.concatenate([jnp.pad(vb, ((0, 0), (1, 0), (0, 0), (0, 0), (0, 0)))[:, :-1], vb], axis=2)
    s = jnp.einsum('bnqhgd,bnlhd->bhgnql', qb, kband, preferred_element_type=jnp.float32) * scale
    koff = jnp.arange(2 * W) - W
    dist = jnp.arange(W)[:, None] - koff[None, :]
    bias = table[t5_bucket(dist)].transpose(2, 0, 1).reshape(hkv, grp, 1, W, 2 * W)
    s = s + bias.astype(jnp.float32)
    kpos = jnp.arange(nb)[:, None, None] * W + koff[None, None, :]
    mask = (dist >= 0)[None] & (dist < W)[None] & (kpos >= 0)
    s = jnp.where(mask, s, NEG)
    sink_col = jnp.broadcast_to(sinks.astype(jnp.float32).reshape(1, hkv, grp, 1, 1, 1), s.shape[:-1] + (1,))
    p = jax.nn.softmax(jnp.concatenate([s, sink_col], axis=-1), axis=-1)[..., :-1].astype(v.dtype)
    out = jnp.einsum('bhgnql,bnlhd->bnqhgd', p, vband)
    return out.reshape(bsz, seq, hq * dh)


def rglru_branch(xc, conv_w, conv_b, w_r, b_r, w_i, b_i, lam):
    bsz, seq, ch = xc.shape
    conv = lax.conv_general_dilated(xc, conv_w.reshape(CONV_WIDTH, 1, ch), window_strides=(1,),
                                    padding=[(CONV_WIDTH - 1, 0)], dimension_numbers=('NWC', 'WIO', 'NWC'),
                                    feature_group_count=ch) + conv_b
    xb = conv.reshape(bsz, seq, C_BLOCKS, C_BLOCK_DIM)
    r = jax.nn.sigmoid((jnp.einsum('bsnc,ncd->bsnd', xb, w_r).reshape(bsz, seq, ch) + b_r).astype(jnp.float32))
    i = jax.nn.sigmoid((jnp.einsum('bsnc,ncd->bsnd', xb, w_i).reshape(bsz, seq, ch) + b_i).astype(jnp.float32))
    log_a = -LRU_C * r * jax.nn.softplus(-lam.astype(jnp.float32))
    a = jnp.exp(log_a)
    b = jnp.sqrt(jnp.maximum(-jnp.expm1(2.0 * log_a), 0.0)) * (i * conv.astype(jnp.float32))

    def combine(left, right):
        a1, b1 = left
        a2, b2 = right
        return a1 * a2, a2 * b1 + b2

    _, h = lax.associative_scan(combine, (a, b), axis=1)
    return h.astype(xc.dtype)


def mixer_layer(x, p_i, rpe_table, norm_g, w_in, sinks, conv_w, conv_b, w_r, b_r, w_i, b_i, lam,
                w_br, w_out, ple_norm_g, w_pg, w_pp):
    bsz, seq, _ = x.shape
    h = rmsnorm(x, norm_g)
    proj = h @ w_in
    cuts = [int(c) for c in np.cumsum(IN_SPLIT_SIZES)[:-1]]
    qa, ka, va, ga, qb, kb, vb, gb, xc, gc, mg = jnp.split(proj, cuts, axis=-1)
    ya = moba_attention(qa.reshape(bsz, seq, A_HEADS, A_HEAD_DIM), ka.reshape(bsz, seq, A_HEADS, A_HEAD_DIM),
                        va.reshape(bsz, seq, A_HEADS, A_HEAD_DIM), rpe_table[:, :A_HEADS].T)
    yb = swa_attention(qb.reshape(bsz, seq, B_Q_HEADS, B_HEAD_DIM), kb.reshape(bsz, seq, B_KV_HEADS, B_HEAD_DIM),
                       vb.reshape(bsz, seq, B_KV_HEADS, B_HEAD_DIM), sinks, rpe_table[:, A_HEADS:])
    yc = rglru_branch(xc, conv_w, conv_b, w_r, b_r, w_i, b_i, lam)
    y = jnp.stack([ya * jax.nn.silu(ga), yb * jax.nn.silu(gb), yc * jax.nn.silu(gc)], axis=2)
    y_d = jnp.einsum('bsnc,ncd->bsnd', y, w_br)
    gates = jax.nn.sigmoid(mg.reshape(bsz, seq, N_BRANCH, D_MODEL))
    merged = jnp.einsum('bsnd,bsnd->bsd', gates, y_d)
    x = x + merged @ w_out
    x = x + jax.nn.sigmoid(rmsnorm(x, ple_norm_g) @ w_pg) * (p_i @ w_pp)
    return x


def setup_inputs(seed: int = 0) -> dict:
    key = jax.random.key(seed)
    ks = jax.random.split(key, 20)
    f32 = jnp.float32
    nrm = lambda k, shape, s: jax.random.normal(k, shape, f32) * s
    a8 = jax.random.uniform(ks[12], (DEPTH, BRANCH_WIDTH), f32, 0.9, 0.999)
    a = a8 ** (1.0 / LRU_C)
    lam = jnp.log(a) - jnp.log1p(-a)
    return {
        'x': nrm(ks[0], (BATCH, SEQ, D_MODEL), 1.0),
        'p': nrm(ks[1], (DEPTH, BATCH, SEQ, PLE_DIM), 1.0),
        'rpe_table': nrm(ks[2], (RPE_BUCKETS, RPE_HEADS), 0.1),
        'norm_g': 1.0 + nrm(ks[3], (DEPTH, D_MODEL), 0.02),
        'w_in': nrm(ks[4], (DEPTH, D_MODEL, IN_WIDTH), D_MODEL ** -0.5),
        'sinks': nrm(ks[5], (DEPTH, B_Q_HEADS), 0.5),
        'conv_w': nrm(ks[6], (DEPTH, CONV_WIDTH, BRANCH_WIDTH), CONV_WIDTH ** -0.5),
        'conv_b': nrm(ks[7], (DEPTH, BRANCH_WIDTH), 0.01),
        'w_r': nrm(ks[8], (DEPTH, C_BLOCKS, C_BLOCK_DIM, C_BLOCK_DIM), C_BLOCK_DIM ** -0.5),
        'b_r': nrm(ks[9], (DEPTH, BRANCH_WIDTH), 0.01),
        'w_i': nrm(ks[10], (DEPTH, C_BLOCKS, C_BLOCK_DIM, C_BLOCK_DIM), C_BLOCK_DIM ** -0.5),
        'b_i': nrm(ks[11], (DEPTH, BRANCH_WIDTH), 0.01),
        'lam': lam,
        'w_br': nrm(ks[13], (DEPTH, N_BRANCH, BRANCH_WIDTH, D_MODEL), BRANCH_WIDTH ** -0.5),
        'w_out': nrm(ks[14], (DEPTH, D_MODEL, D_MODEL), D_MODEL ** -0.5),
        'ple_norm_g': 1.0 + nrm(ks[15], (DEPTH, D_MODEL), 0.02),
        'w_pg': nrm(ks[16], (DEPTH, D_MODEL, D_MODEL), D_MODEL ** -0.5),
        'w_pp': nrm(ks[17], (DEPTH, PLE_DIM, D_MODEL), PLE_DIM ** -0.5),
        'final_norm_g': 1.0 + nrm(ks[18], (D_MODEL,), 0.02),
    }


def reference(x, p, rpe_table, norm_g, w_in, sinks, conv_w, conv_b, w_r, b_r, w_i, b_i, lam,
              w_br, w_out, ple_norm_g, w_pg, w_pp, final_norm_g):
    for i in range(DEPTH):
        x = mixer_layer(x, p[i], rpe_table, norm_g[i], w_in[i], sinks[i], conv_w[i], conv_b[i],
                        w_r[i], b_r[i], w_i[i], b_i[i], lam[i], w_br[i], w_out[i],
                        ple_norm_g[i], w_pg[i], w_pp[i])
    return rmsnorm(x, final_norm_g)
```

```python
import math
from contextlib import ExitStack

import numpy as np
import concourse.bass as bass
import concourse.mybir as mybir
from concourse.bass_utils import run_bass_kernel_spmd

F32 = mybir.dt.float32
BF16 = mybir.dt.bfloat16
AF = mybir.ActivationFunctionType
ALU = mybir.AluOpType
AX = mybir.AxisListType

D = 2048
INW = 14848
PLE = 256
EPS = 1e-6
OFF = dict(qa=0, ka=1024, va=2048, ga=3072, qb=4096, kb=5120, vb=5376, gb=5632,
           xc=6656, gc=7680, mg=8704)
NEGM = -30000.0
BIG = 1.0e30
ENGS = ("sync", "act", "dve", "pool", "pe")


class Prog:
    def __init__(self, nc, es, n_dma_sems=64):
        self.nc = nc
        self.semh = {}
        for e in ("act", "dve", "pool", "pe"):
            self.semh["c_" + e] = es.enter_context(nc.semaphore("c_" + e))
        self.all_dma = []
        for i in range(n_dma_sems):
            self.semh[f"dq{i}"] = es.enter_context(nc.semaphore(f"dq{i}"))
            self.all_dma.append(f"dq{i}")
        self.free_dma = list(self.all_dma)
        self.key2sem = {}
        self.dma_cnt = {s: 0 for s in self.all_dma}
        self.cnt = {e: 0 for e in ENGS}
        self.unsig = {e: False for e in ENGS}
        self.seen = {e: {} for e in ENGS}
        self.ops = {e: [] for e in ENGS}
        self.state = {}

    def _st(self, b):
        s = self.state.get(b)
        if s is None:
            s = self.state[b] = {"w": None, "r": []}
        return s

    def _deps(self, eng, reads, writes):
        deps = {}
        def add(tok):
            if tok is None:
                return
            s, v = tok
            if eng == "pe" and s == "c_pe":
                return
            if deps.get(s, 0) < v:
                deps[s] = v
        for b in reads:
            add(self._st(b)["w"])
        for b in writes:
            st = self._st(b)
            add(st["w"])
            for t in st["r"]:
                add(t)
        out = []
        for s, v in deps.items():
            if self.seen[eng].get(s, 0) < v:
                self.seen[eng][s] = v
                out.append((s, v))
        return out

    def _commit(self, tok, reads, writes):
        for b in reads:
            self._st(b)["r"].append(tok)
        for b in writes:
            st = self._st(b)
            st["w"] = tok
            st["r"] = []

    def op(self, eng, fn, reads=(), writes=(), signal=True):
        waits = self._deps(eng, reads, writes)
        s = "c_" + eng
        if signal:
            self.cnt[eng] += 1
            tok = (s, self.cnt[eng])
            self.unsig[eng] = False
            inc = (s, 1)
        else:
            tok = (s, self.cnt[eng] + 1)
            self.unsig[eng] = True
            inc = None
        self.ops[eng].append((waits, fn, inc))
        self._commit(tok, reads, writes)

    def dma(self, q, out, in_, reads=(), writes=(), key=None):
        assert key is not None
        waits = self._deps(q, reads, writes)
        if key not in self.key2sem:
            self.key2sem[key] = self.free_dma.pop()
        s = self.key2sem[key]
        self.dma_cnt[s] += 16
        tok = (s, self.dma_cnt[s])
        self.ops[q].append((waits, (lambda e, o=out, i=in_: e.dma_start(out=o, in_=i)), (s, 16)))
        self._commit(tok, reads, writes)

    def barrier(self):
        for e in ENGS:
            assert not self.unsig[e], e
        for e in ENGS:
            waits = []
            for f in ("act", "dve", "pool", "pe"):
                s = "c_" + f
                if f != e and self.cnt[f] > self.seen[e].get(s, 0):
                    self.seen[e][s] = self.cnt[f]
                    waits.append((s, self.cnt[f]))
            for s, v in self.dma_cnt.items():
                if v > self.seen[e].get(s, 0):
                    self.seen[e][s] = v
                    waits.append((s, v))
            if waits:
                self.ops[e].append((waits, None, None))
        self.state = {}
        self.key2sem = {}
        self.free_dma = list(self.all_dma)

    def flush(self):
        self.barrier()
        nc = self.nc
        semh = self.semh
        with nc.Block() as block:
            decos = dict(sync=block.sync, act=block.scalar, dve=block.vector,
                         pool=block.gpsimd, pe=block.tensor)
            for name in ENGS:
                ops = self.ops[name]
                if not ops:
                    continue
                def body(eng, ops=ops):
                    for waits, fn, inc in ops:
                        for s, v in waits:
                            eng.wait_ge(semh[s], v)
                        if fn is not None:
                            ins = fn(eng)
                            if inc is not None:
                                ins.then_inc(semh[inc[0]], inc[1])
                decos[name](body)
        self.ops = {e: [] for e in ENGS}


_UID = [0]


def _uniq(name):
    _UID[0] += 1
    return "%s_%d" % (name, _UID[0])


def sb(es, nc, name, shape, dt):
    return es.enter_context(nc.sbuf_tensor(_uniq(name), list(shape), dt))


def ps(es, nc, name, shape, dt):
    return es.enter_context(nc.psum_tensor(_uniq(name), list(shape), dt))


def MM(out, lhsT, rhs, start, stop):
    return lambda e: e.matmul(out, lhsT, rhs, start=start, stop=stop)


def TR(out, in_, ident):
    return lambda e: e.transpose(out, in_, ident)


def ACT(out, in_, func, bias=None, scale=None, accum_out=None):
    kw = {}
    if bias is not None:
        kw["bias"] = bias
    if scale is not None:
        kw["scale"] = scale
    if accum_out is not None:
        kw["accum_out"] = accum_out
    return lambda e: e.activation(out=out, in_=in_, func=func, **kw)


def TT(out, in0, in1, op):
    return lambda e: e.tensor_tensor(out=out, in0=in0, in1=in1, op=op)


def TS(out, in0, s1, s2, op0, op1=None):
    if op1 is None:
        return lambda e: e.tensor_scalar(out=out, in0=in0, scalar1=s1, scalar2=None, op0=op0)
    return lambda e: e.tensor_scalar(out=out, in0=in0, scalar1=s1, scalar2=s2, op0=op0, op1=op1)


def STT(out, in0, scalar, in1, op0, op1, accum_out=None):
    if accum_out is None:
        return lambda e: e.scalar_tensor_tensor(out=out, in0=in0, scalar=scalar, in1=in1, op0=op0, op1=op1)
    return lambda e: e.scalar_tensor_tensor(out=out, in0=in0, scalar=scalar, in1=in1, op0=op0, op1=op1,
                                            accum_out=accum_out)


def CP(out, in_):
    return lambda e: e.tensor_copy(out=out, in_=in_)


class Ctx:
    pass


def build_program(T=4096, NL=2, debug=False):
    assert T % 512 == 0 and T // 256 <= 16
    nc = bass.Bass("TRN2", target_bir_lowering=False)
    C = Ctx()
    C.T = T
    NT128 = T // 128
    NT512 = T // 512
    NBLK = T // 256

    def din(name, shape, dt=F32):
        return nc.dram_tensor(name, list(shape), dt, kind="ExternalInput").ap()

    x = din("x", [T, D])
    p = din("p", [NL, T, PLE])
    norm_g = din("norm_g", [NL, D])
    w_in = din("w_in", [NL, D, INW])
    sinks = din("sinks", [NL, 16])
    cvec = din("cvec", [NL, 128, 8, 8])
    w_r = din("w_r", [NL, 8, 128, 128])
    w_i = din("w_i", [NL, 8, 128, 128])
    w_br = din("w_br", [NL, 3, 1024, D])
    w_out = din("w_out", [NL, D, D])
    ple_g = din("ple_norm_g", [NL, D])
    w_pg = din("w_pg", [NL, D, D])
    w_pp = din("w_pp", [NL, PLE, D])
    fin_g = din("final_norm_g", [D])
    identf = din("identf", [128, 128])
    emat = din("emat", [16, 16 * 128])
    pastm = din("pastm", [128, NT128 * 16])
    biasA = din("biasA", [8, 128, 6, 512])
    maskA = din("maskA", [128, 6, 512])
    c31 = din("c31", [128, 8])
    biasB = din("biasB", [128, 2, 16, 128])
    maskB = din("maskB", [128, 2, 16, 128])
    y = nc.dram_tensor("y", [T, D], F32, kind="ExternalOutput").ap()

    skind = "ExternalOutput" if debug else "Internal"

    def dscr(name, shape, dt):
        return nc.dram_tensor(name, list(shape), dt, kind=skind).ap()

    qaT = dscr("qaT", [1024, T], BF16)
    kaT = dscr("kaT", [1024, T], BF16)
    va = dscr("va", [T, 1024], BF16)
    gaT = dscr("gaT", [1024, T], BF16)
    qbT = dscr("qbT", [1024, T], BF16)
    kbT = dscr("kbT", [256, T], BF16)
    vb = dscr("vb", [T, 256], BF16)
    gbT = dscr("gbT", [1024, T], BF16)
    xcT = dscr("xcT", [1024, T], F32)
    gcT = dscr("gcT", [1024, T], BF16)
    mgT = dscr("mgT", [3 * D, T], BF16)
    ygT = dscr("ygT", [3 * 1024, T], BF16)
    mrgT = dscr("mrgT", [D, T], BF16)
    xm = dscr("xm", [T, D], F32)
    xs = [dscr(f"xs{l}", [T, D], F32) for l in range(NL)]

    with ExitStack() as top:
        P = Prog(nc, top)
        ident = sb(top, nc, "ident", [128, 128], BF16)
        ones = sb(top, nc, "ones", [128, 128], BF16)
        embf = sb(top, nc, "embf", [128, 16 * 128], BF16)
        mhalf = sb(top, nc, "mhalf", [128, 1], F32)

        with ExitStack() as es:
            idf = sb(es, nc, "S_idf", [128, 128], F32)
            emf = sb(es, nc, "S_emf", [16, 16 * 128], F32)
            P.dma("sync", idf[:], identf[:, :], writes=["idf"], key="s0")
            P.dma("sync", emf[:], emat[:, :], writes=["emf"], key="s1")
            P.op("dve", CP(ident[:], idf[:]), reads=["idf"], writes=["ident"])
            P.op("pool", lambda e: e.memset(embf[:], 0.0), writes=["embf"])
            P.op("dve", CP(embf[0:16, :], emf[:]), reads=["emf", "embf"], writes=["embf"])
            P.op("pool", lambda e: e.memset(ones[:], 1.0), writes=["ones"])
            P.op("pool", lambda e: e.memset(mhalf[:], -0.5), writes=["mhalf"])
            P.flush()

        def phase_A(l, xsrc, hT):
            with ExitStack() as es:
                xin = [sb(es, nc, f"A_x{i}", [128, D], F32) for i in range(3)]
                hb = [sb(es, nc, f"A_hb{i}", [128, D], BF16) for i in range(2)]
                jk = sb(es, nc, "A_jk", [128, D], BF16)
                gbc = sb(es, nc, "A_g", [128, D], F32)
                ssum = sb(es, nc, "A_ss", [128, NT128], F32)
                sq = sb(es, nc, "A_sq", [128, NT128], F32)
                rsd = sb(es, nc, "A_rs", [128, NT128], F32)
                pt = [ps(es, nc, f"A_pt{i}", [128, 1024], BF16) for i in range(4)]
                P.dma("sync", gbc[:], norm_g[l].partition_broadcast(128), writes=["Ag"], key="Ag")
                def loadA(i):
                    P.dma("sync", xin[i % 3][:], xsrc[i * 128:(i + 1) * 128, :],
                          writes=["Ax%d" % (i % 3)], key="Ax%d" % (i % 3))

                def normA1(i):
                    xt_ = xin[i % 3]
                    P.op("dve", STT(jk[:], xt_[:], 1.0, xt_[:], ALU.mult, ALU.mult, accum_out=ssum[:, i:i + 1]),
                         reads=["Ax%d" % (i % 3)], writes=["Ajk", "Ass%d" % i])
                    P.op("dve", TS(ssum[:, i:i + 1], ssum[:, i:i + 1], 1.0 / D, EPS, ALU.mult, ALU.add),
                         reads=["Ass%d" % i], writes=["Ass%d" % i])
                    P.op("pool", TT(rsd[:, i:i + 1], ssum[:, i:i + 1], mhalf[:, 0:1], ALU.pow),
                         reads=["Ass%d" % i, "mhalf"], writes=["Ars%d" % i])

                def normA2(i):
                    xt_ = xin[i % 3]
                    P.op("dve", STT(hb[i % 2][:], xt_[:], rsd[:, i:i + 1], gbc[:], ALU.mult, ALU.mult),
                         reads=["Ax%d" % (i % 3), "Ars%d" % i, "Ag"], writes=["Ahb%d" % (i % 2)])

                loadA(0)
                if NT128 > 1:
                    loadA(1)
                normA1(0)
                for i in range(NT128):
                    if i + 2 < NT128:
                        loadA(i + 2)
                    if i + 1 < NT128:
                        normA1(i + 1)
                    normA2(i)
                    for half in range(2):
                        pti = (2 * i + half) % 4
                        for j in range(8):
                            kc = half * 8 + j
                            P.op("pe", TR(pt[pti][:, j * 128:(j + 1) * 128], hb[i % 2][:, kc * 128:(kc + 1) * 128], ident[:]),
                                 reads=["Ahb%d" % (i % 2), "ident"], writes=["Apt%d" % pti], signal=(j == 7))
                        P.op("act", ACT(hT[:, half * 8:(half + 1) * 8, i * 128:(i + 1) * 128],
                                        pt[pti][:].rearrange("p (j t) -> p j t", t=128), AF.Copy),
                             reads=["Apt%d" % pti], writes=["hT"])
                P.flush()

        def norm_tile_eps(P_, tag, xt, gbc, hb, ssum, sq, rsd, jk, i, xkey=None):
            xkey = xkey or (tag + "x%d" % (i % 2))
            P_.op("dve", STT(jk[:], xt[:], 1.0, xt[:], ALU.mult, ALU.mult, accum_out=ssum[:, i:i + 1]),
                  reads=[xkey], writes=[tag + "jk", tag + "ss%d" % i])
            P_.op("dve", TS(ssum[:, i:i + 1], ssum[:, i:i + 1], 1.0 / D, EPS, ALU.mult, ALU.add),
                  reads=[tag + "ss%d" % i], writes=[tag + "ss%d" % i])
            P_.op("pool", TT(rsd[:, i:i + 1], ssum[:, i:i + 1], mhalf[:, 0:1], ALU.pow),
                  reads=[tag + "ss%d" % i, "mhalf"], writes=[tag + "rs%d" % i])
            P_.op("dve", STT(hb[:], xt[:], rsd[:, i:i + 1], gbc[:], ALU.mult, ALU.mult),
                  reads=[xkey, tag + "rs%d" % i, tag + "g"],
                  writes=[tag + "hb%d" % (i % 2)])

        def phase_B(l, hT):
            jobs = []
            def fm(name, dest, func, scale=None, dt=BF16, width=None):
                c0 = OFF[name]
                width_ = width or 1024
                for s in range(0, width_, 512):
                    n = min(512, width_ - s)
                    jobs.append(dict(c0=c0 + s, n=n, kind="fm", dest=dest, r0=s, func=func, scale=scale, dt=dt))
            def tm(name, dest, width):
                c0 = OFF[name]
                for s in range(0, width, 512):
                    n = min(512, width - s)
                    jobs.append(dict(c0=c0 + s, n=n, kind="tm", dest=dest, r0=s))
            fm("qa", qaT, AF.Copy, scale=128 ** -0.5)
            fm("ka", kaT, AF.Copy)
            tm("va", va, 1024)
            fm("ga", gaT, AF.Silu)
            fm("qb", qbT, AF.Copy, scale=0.125)
            fm("kb", kbT, AF.Copy, width=256)
            tm("vb", vb, 256)
            fm("gb", gbT, AF.Silu)
            fm("xc", xcT, AF.Copy, dt=F32)
            fm("gc", gcT, AF.Silu)
            fm("mg", mgT, AF.Sigmoid, width=3 * D)
            TH = min(T, 2048)
            with ExitStack() as es:
                wt = [sb(es, nc, f"B_w{i}", [128, 16, 512], BF16) for i in range(2)]
                obb = [sb(es, nc, f"B_ob{i}", [128, TH], BF16) for i in range(2)]
                obf = [sb(es, nc, f"B_of{i}", [128, TH], F32) for i in range(2)]
                obt = [sb(es, nc, f"B_ot{i}", [128, 4, 512], BF16) for i in range(2)]
                pb = [ps(es, nc, f"B_p{i}", [128, 512], F32) for i in range(4)]
                pk = 0
                kb_ = 0
                kf_ = 0
                kt_ = 0
                wview = w_in[l].rearrange("(kc p) n -> p kc n", p=128)
                for ji, jb in enumerate(jobs):
                    w = wt[ji % 2]
                    wk = "Bw%d" % (ji % 2)
                    n = jb["n"]
                    P.dma("pool", w[:, :, 0:n], wview[:, :, jb["c0"]:jb["c0"] + n], writes=[wk], key=wk)
                    if jb["kind"] == "fm":
                        for cc in range(n // 128):
                            for half in range(T // TH):
                                if jb["dt"] == F32:
                                    ob = obf[kf_ % 2]; obk = "Bof%d" % (kf_ % 2); kf_ += 1
                                else:
                                    ob = obb[kb_ % 2]; obk = "Bob%d" % (kb_ % 2); kb_ += 1
                                for tt in range(TH // 512):
                                    t0 = half * TH + tt * 512
                                    pbk = "Bp%d" % (pk % 4); pbt = pb[pk % 4]; pk += 1
                                    for kc in range(16):
                                        P.op("pe", MM(pbt[:, :], w[:, kc, cc * 128:(cc + 1) * 128], hT[:, kc, t0:t0 + 512],
                                                      kc == 0, kc == 15),
                                             reads=[wk, "hT"], writes=[pbk], signal=(kc == 15))
                                    P.op("act", ACT(ob[:, tt * 512:(tt + 1) * 512], pbt[:, :], jb["func"], scale=jb["scale"]),
                                         reads=[pbk], writes=[obk])
                                r = jb["r0"] + cc * 128
                                P.dma("sync", jb["dest"][r:r + 128, half * TH:(half + 1) * TH], ob[:, :],
                                      reads=[obk], key=obk)
                    else:
                        for tg in range(T // 512):
                            ot = obt[kt_ % 2]; otk = "Bot%d" % (kt_ % 2); kt_ += 1
                            for ti in range(4):
                                tok0 = tg * 512 + ti * 128
                                pbk = "Bp%d" % (pk % 4); pbt = pb[pk % 4]; pk += 1
                                for kc in range(16):
                                    P.op("pe", MM(pbt[:, 0:n], hT[:, kc, tok0:tok0 + 128], w[:, kc, 0:n], kc == 0, kc == 15),
                                         reads=[wk, "hT"], writes=[pbk], signal=(kc == 15))
                                P.op("act", ACT(ot[:, ti, 0:n], pbt[:, 0:n], AF.Copy), reads=[pbk], writes=[otk])
                            c = jb["r0"]
                            P.dma("sync", jb["dest"][tg * 512:(tg + 1) * 512, c:c + n].rearrange("(a q) c -> q a c", q=128),
                                  ot[:, :, 0:n], reads=[otk], key=otk)
                P.flush()

        def phase_C(l):
            with ExitStack() as es:
                cv = sb(es, nc, "C_cv", [128, 8, 8], F32)
                e1 = sb(es, nc, "C_e1", [128, 8], F32)
                n8 = sb(es, nc, "C_n8", [128, 8], F32)
                n16 = sb(es, nc, "C_n16", [128, 8], F32)
                wr = sb(es, nc, "C_wr", [128, 8, 128], BF16)
                wi = sb(es, nc, "C_wi", [128, 8, 128], BF16)
                xt1 = sb(es, nc, "C_x", [128, T], F32)
                gct = [sb(es, nc, f"C_g{i}", [128, T], BF16) for i in range(2)]
                ot1 = sb(es, nc, "C_o", [128, T], BF16)
                u2 = [sb(es, nc, f"C_u{i}", [128, T], F32) for i in range(2)]
                ub1 = sb(es, nc, "C_ub", [128, T], BF16)
                rt2 = [sb(es, nc, f"C_r{i}", [128, T], F32) for i in range(2)]
                it2 = [sb(es, nc, f"C_i{i}", [128, T], F32) for i in range(2)]
                at2 = [sb(es, nc, f"C_a{i}", [128, T], F32) for i in range(2)]
                ht = sb(es, nc, "C_h", [128, T], F32)
                pr = [ps(es, nc, f"C_pr{i}", [128, 512], F32) for i in range(2)]
                pi = [ps(es, nc, f"C_pi{i}", [128, 512], F32) for i in range(2)]
                P.dma("sync", cv[:], cvec[l], writes=["cv"], key="Ccv")
                P.dma("pool", wr[:], w_r[l].rearrange("n c d -> c n d"), writes=["wr"], key="Cwr")
                P.dma("pool", wi[:], w_i[l].rearrange("n c d -> c n d"), writes=["wi"], key="Cwi")
                P.op("act", ACT(e1[:], cv[:, :, 7], AF.Exp, scale=-1.0), reads=["cv"], writes=["e1"])
                P.op("act", ACT(e1[:], e1[:], AF.Ln, bias=1.0), reads=["e1"], writes=["e1"])
                P.op("dve", TS(n8[:], e1[:], -8.0, None, ALU.mult), reads=["e1"], writes=["n8"])
                P.op("dve", TS(n16[:], e1[:], -16.0, None, ALU.mult), reads=["e1"], writes=["n16"])

                def loadX(n):
                    P.dma("sync", xt1[:], xcT[n * 128:(n + 1) * 128, :], writes=["Cx"], key="Cx")

                def loadG(n):
                    b = n % 2
                    P.dma("sync", gct[b][:], gcT[n * 128:(n + 1) * 128, :], writes=["Cg%d" % b], key="Cg%d" % b)

                def stage1(n):
                    b = n % 2
                    X = xt1
                    u = u2[b]
                    uk = "u%d" % b
                    P.op("dve", TS(u[:], X[:], cv[:, n, 3:4], cv[:, n, 4:5], ALU.mult, ALU.add),
                         reads=["Cx", "cv"], writes=[uk])
                    for s_, wtap in ((1, 2), (2, 1), (3, 0)):
                        P.op("dve", STT(u[:, s_:T], X[:, 0:T - s_], cv[:, n, wtap:wtap + 1], u[:, s_:T], ALU.mult, ALU.add),
                             reads=["Cx", "cv", uk], writes=[uk])

                def stage2a(n):
                    b = n % 2
                    rt, it, at = rt2[b], it2[b], at2[b]
                    rk, ik, ak = "r%d" % b, "i%d" % b, "a%d" % b
                    P.op("act", ACT(ub1[:], u2[b][:], AF.Copy), reads=["u%d" % b], writes=["ub"])
                    for tt in range(NT512):
                        sl = slice(tt * 512, (tt + 1) * 512)
                        k = tt % 2
                        P.op("pe", MM(pr[k][:, :], wr[:, n, :], ub1[:, sl], True, True), reads=["wr", "ub"], writes=["Cpr%d" % k])
                        P.op("pe", MM(pi[k][:, :], wi[:, n, :], ub1[:, sl], True, True), reads=["wi", "ub"], writes=["Cpi%d" % k])
                        P.op("act", ACT(rt[:, sl], pr[k][:, :], AF.Sigmoid, bias=cv[:, n, 5:6]),
                             reads=["Cpr%d" % k, "cv"], writes=[rk])
                        P.op("act", ACT(it[:, sl], pi[k][:, :], AF.Sigmoid, bias=cv[:, n, 6:7]),
                             reads=["Cpi%d" % k, "cv"], writes=[ik])
                    P.op("pool", TT(it[:], it[:], u2[b][:], ALU.mult), reads=[ik, "u%d" % b], writes=[ik])
                    P.op("act", ACT(at[:], rt[:], AF.Exp, scale=n8[:, n:n + 1]), reads=[rk, "n8"], writes=[ak])
                    P.op("act", ACT(rt[:], rt[:], AF.Exp, scale=n16[:, n:n + 1]), reads=[rk, "n16"], writes=[rk])
                    P.op("act", ACT(rt[:], rt[:], AF.Sqrt, bias=1.0, scale=-1.0), reads=[rk], writes=[rk])

                def stage2b(n):
                    b = n % 2
                    rt, it, at = rt2[b], it2[b], at2[b]
                    rk, ik, ak = "r%d" % b, "i%d" % b, "a%d" % b
                    P.op("dve", TT(it[:], it[:], rt[:], ALU.mult), reads=[ik, rk], writes=[ik])
                    P.op("dve", lambda e, a_=at, i_=it, h_=ht: e.tensor_tensor_scan(out=h_[:], data0=a_[:], data1=i_[:], initial=0.0,
                                                                                   op0=ALU.mult, op1=ALU.add),
                         reads=[ak, ik], writes=["h"])
                    P.op("pool", TT(ot1[:], ht[:], gct[b][:], ALU.mult), reads=["h", "Cg%d" % b], writes=["Co"])
                    P.dma("sync", ygT[2048 + n * 128:2048 + (n + 1) * 128, :], ot1[:], reads=["Co"], key="Co")

                loadX(0)
                loadG(0)
                stage1(0)
                loadX(1)
                stage2a(0)
                stage1(1)
                for n in range(8):
                    if n + 2 < 8:
                        loadX(n + 2)
                    if n + 1 < 8:
                        loadG(n + 1)
                        stage2a(n + 1)
                    if n + 2 < 8:
                        stage1(n + 2)
                    stage2b(n)
                P.flush()

        def phase_D(l):
            with ExitStack() as es:
                bBf = sb(es, nc, "D_bf", [128, 2, 16, 128], F32)
                mBf = sb(es, nc, "D_mf", [128, 2, 16, 128], F32)
                bB = sb(es, nc, "D_bb", [128, 2, 16, 128], BF16)
                sk = sb(es, nc, "D_sk", [128, 16], F32)
                esk = sb(es, nc, "D_es", [128, 16], F32)
                idf32 = sb(es, nc, "D_idf", [128, 128], F32)
                qg2 = [sb(es, nc, f"D_q{i}", [64, 4, T], BF16) for i in range(2)]
                gg2 = sb(es, nc, "D_g", [128, 2, T], BF16)
                og2 = sb(es, nc, "D_o", [128, 2, T], BF16)
                kg2 = [sb(es, nc, f"D_k{i}", [64, T], BF16) for i in range(2)]
                vaug2 = [sb(es, nc, f"D_v{i}", [128, NT128, 65], BF16) for i in range(2)]
                pto = [sb(es, nc, f"D_pto{i}", [128, 4, 128], BF16) for i in range(2)]
                ptp = [sb(es, nc, f"D_ptp{i}", [128, 4, 128], BF16) for i in range(2)]
                dn = [sb(es, nc, f"D_dn{i}", [128, 4], F32) for i in range(2)]
                rdn = [sb(es, nc, f"D_rdn{i}", [128, 4], F32) for i in range(2)]
                yn = [sb(es, nc, f"D_yn{i}", [128, 256], F32) for i in range(2)]
                sto = [ps(es, nc, f"D_so{i}", [128, 512], F32) for i in range(2)]
                stp = [ps(es, nc, f"D_sp{i}", [128, 512], F32) for i in range(2)]
                accp = [ps(es, nc, f"D_ac{i}", [128, 512], F32) for i in range(2)]
                ytr = [ps(es, nc, f"D_yt{i}", [128, 512], F32) for i in range(2)]
                P.dma("sync", bBf[:], biasB[:, :, :, :], writes=["bBf"], key="Dbf")
                P.dma("sync", mBf[:], maskB[:, :, :, :], writes=["mBf"], key="Dmf")
                P.dma("sync", sk[:], sinks[l].partition_broadcast(128), writes=["sk"], key="Dsk")
                P.dma("sync", idf32[:], identf[:, :], writes=["idf32"], key="Did")
                P.op("dve", TT(bB[:], bBf[:], mBf[:], ALU.add), reads=["bBf", "mBf"], writes=["bB"])
                P.op("act", ACT(esk[:], sk[:], AF.Exp), reads=["sk"], writes=["esk"])
                for i_ in range(2):
                    P.op("pool", lambda e, i_=i_: e.memset(vaug2[i_][:, :, 64:65], 1.0), writes=["vg%d" % i_])

                def emit_st(g, j):
                    b = j % 2
                    qg, kg = qg2[g % 2], kg2[g % 2]
                    qs = slice(j * 128, (j + 1) * 128)
                    so = sto[b][:].rearrange("p (h q) -> p h q", q=128)
                    P.op("pe", MM(so, kg[:, qs], qg[:, :, qs], True, False), reads=["kg%d" % (g % 2), "qg%d" % (g % 2)], writes=["Dso%d" % b], signal=False)
                    P.op("pe", MM(so, ident[:], bB[:, 1, 4 * g:4 * g + 4, :], False, True), reads=["ident", "bB"], writes=["Dso%d" % b])
                    P.op("act", ACT(pto[b][:], so, AF.Exp), reads=["Dso%d" % b], writes=["Dpto%d" % b])
                    if j > 0:
                        sp = stp[b][:].rearrange("p (h q) -> p h q", q=128)
                        ks = slice((j - 1) * 128, j * 128)
                        P.op("pe", MM(sp, kg[:, ks], qg[:, :, qs], True, False), reads=["kg%d" % (g % 2), "qg%d" % (g % 2)], writes=["Dsp%d" % b], signal=False)
                        P.op("pe", MM(sp, ident[:], bB[:, 0, 4 * g:4 * g + 4, :], False, True), reads=["ident", "bB"], writes=["Dsp%d" % b])
                        P.op("act", ACT(ptp[b][:], sp, AF.Exp), reads=["Dsp%d" % b], writes=["Dptp%d" % b])

                def emit_pv(g, j):
                    b = j % 2
                    qs = slice(j * 128, (j + 1) * 128)
                    av = accp[b][:, 0:260].rearrange("p (h c) -> p h c", c=65)
                    ak = "Dac%d" % b
                    vaug = vaug2[g % 2]
                    vgk = "vg%d" % (g % 2)
                    for h in range(4):
                        if j > 0:
                            P.op("pe", (lambda e, av=av, h=h, b=b, j=j, vaug=vaug: e.matmul(av[:, h, :], ptp[b][:, h, :], vaug[:, j - 1, :],
                                                                             start=(h == 0), stop=False, skip_group_check=True)),
                                 reads=[vgk, "Dptp%d" % b], writes=[ak], signal=False)
                            P.op("pe", (lambda e, av=av, h=h, b=b, j=j, vaug=vaug: e.matmul(av[:, h, :], pto[b][:, h, :], vaug[:, j, :],
                                                                             start=False, stop=True, skip_group_check=True)),
                                 reads=[vgk, "Dpto%d" % b], writes=[ak], signal=(h == 3))
                        else:
                            P.op("pe", (lambda e, av=av, h=h, b=b, j=j, vaug=vaug: e.matmul(av[:, h, :], pto[b][:, h, :], vaug[:, j, :],
                                                                             start=(h == 0), stop=True, skip_group_check=True)),
                                 reads=[vgk, "Dpto%d" % b], writes=[ak], signal=(h == 3))
                    P.op("dve", TT(dn[b][:], av[:, :, 64], esk[:, 4 * g:4 * g + 4], ALU.add), reads=[ak, "esk"], writes=["Ddn%d" % b])
                    P.op("dve", lambda e, b=b: e.reciprocal(out=rdn[b][:], in_=dn[b][:]), reads=["Ddn%d" % b], writes=["Drdn%d" % b])
                    P.op("dve", TT(yn[b][:].rearrange("p (h c) -> p h c", c=64), av[:, :, 0:64],
                                   rdn[b][:].unsqueeze(2).broadcast_to([128, 4, 64]), ALU.mult),
                         reads=[ak, "Drdn%d" % b], writes=["Dyn%d" % b])

                def emit_tr(g, j):
                    b = j % 2
                    qs = slice(j * 128, (j + 1) * 128)
                    for pr_ in range(2):
                        P.op("pe", TR(ytr[b][:, pr_ * 128:(pr_ + 1) * 128], yn[b][:, pr_ * 128:(pr_ + 1) * 128], idf32[:]),
                             reads=["Dyn%d" % b, "idf32"], writes=["Dyt%d" % b], signal=(pr_ == 1))
                    P.op("dve", TT(og2[:, :, qs], ytr[b][:, 0:256].rearrange("p (r q) -> p r q", q=128), gg2[:, :, qs], ALU.mult),
                         reads=["Dyt%d" % b, "gg"], writes=["og"])

                def loadD(g):
                    i_ = g % 2
                    P.dma("sync", kg2[i_][:], kbT[g * 64:(g + 1) * 64, :], writes=["kg%d" % i_], key="Dk%d" % i_)
                    P.dma("sync", qg2[i_][:], qbT[g * 256:(g + 1) * 256, :].rearrange("(h d) t -> d h t", d=64),
                          writes=["qg%d" % i_], key="Dq%d" % i_)
                    P.dma("sync", vaug2[i_][:, :, 0:64], vb[:, g * 64:(g + 1) * 64].rearrange("(c k) d -> k c d", k=128),
                          writes=["vg%d" % i_], key="Dv%d" % i_)

                loadD(0)
                for g in range(4):
                    P.dma("sync", gg2[:], gbT[g * 256:(g + 1) * 256, :].rearrange("(r q) t -> q r t", q=128), writes=["gg"], key="Dg")
                    if g + 1 < 4:
                        loadD(g + 1)
                    emit_st(g, 0)
                    for j in range(NT128):
                        if j + 1 < NT128:
                            emit_st(g, j + 1)
                        emit_pv(g, j)
                        if j > 0:
                            emit_tr(g, j - 1)
                    emit_tr(g, NT128 - 1)
                    P.dma("sync", ygT[1024 + g * 256:1024 + (g + 1) * 256, :].rearrange("(r q) t -> q r t", q=128), og2[:],
                          reads=["og"], key="Do")
                P.flush()

        def phase_E(l):
            with ExitStack() as es:
                mAf = sb(es, nc, "E_mf", [128, 6, 512], F32)
                bAf = sb(es, nc, "E_bf", [128, 6, 512], F32)
                bA = sb(es, nc, "E_bb", [128, 6, 512], BF16)
                c31s = sb(es, nc, "E_c31", [128, 8], F32)
                pms = sb(es, nc, "E_pm", [128, NT128 * 16], F32)
                idf32 = sb(es, nc, "E_idf", [128, 128], F32)
                qh = [sb(es, nc, f"E_q{i}", [128, T], BF16) for i in range(2)]
                kh = [sb(es, nc, f"E_k{i}", [128, T], BF16) for i in range(2)]
                vh = [sb(es, nc, f"E_v{i}", [128, NT128, 129], BF16) for i in range(2)]
                gh = [sb(es, nc, f"E_g{i}", [128, T], BF16) for i in range(2)]
                oh = [sb(es, nc, f"E_o{i}", [128, T], BF16) for i in range(2)]
                ksum = sb(es, nc, "E_ks", [128, 16], F32)
                kmb = sb(es, nc, "E_km", [128, 16], BF16)
                gm = sb(es, nc, "E_gm", [128, NT128 * 16], F32)
                top8 = sb(es, nc, "E_t8", [128, NT128, 8], F32)
                msel = sb(es, nc, "E_ms", [128, NT128, 16], F32)
                mselT = sb(es, nc, "E_mt", [128, T], BF16)
                pt = [sb(es, nc, f"E_pt{i}", [128, 512], BF16) for i in range(4)]
                rdn = sb(es, nc, "E_rd", [128, 8], F32)
                yn = [sb(es, nc, f"E_yn{i}", [128, 128], F32) for i in range(4)]
                trp = ps(es, nc, "E_trp", [128, 512], F32)
                stp = [ps(es, nc, f"E_st{i}", [128, 512], F32) for i in range(3)]
                gps = stp[2]
                acc = [[ps(es, nc, f"E_acc{i}_{k}", [128, 512], F32) for k in range(2)] for i in range(2)]
                P.dma("sync", mAf[:], maskA[:, :, :], writes=["mAf"], key="Emf")
                P.dma("sync", c31s[:], c31[:, :], writes=["c31"], key="Ec31")
                P.dma("sync", pms[:], pastm[:, :], writes=["pms"], key="Epm")
                P.dma("sync", idf32[:], identf[:, :], writes=["idf32"], key="Eid")
                P.op("pool", lambda e: e.memset(ksum[:], 0.0), writes=["ksum"])
                P.op("pool", lambda e: e.memset(mselT[:], 0.0), writes=["mselT"])
                for i_ in range(2):
                    P.op("pool", lambda e, i_=i_: e.memset(vh[i_][:, :, 128:129], 1.0), writes=["Ev%d" % i_])

                def loadE(h):
                    b = h % 2
                    P.dma("sync", qh[b][:], qaT[h * 128:(h + 1) * 128, :], writes=["Eq%d" % b], key="Eq%d" % b)
                    P.dma("sync", kh[b][:], kaT[h * 128:(h + 1) * 128, :], writes=["Ek%d" % b], key="Ek%d" % b)
                    P.dma("sync", vh[b][:, :, 0:128], va[:, h * 128:(h + 1) * 128].rearrange("(c k) d -> k c d", k=128),
                          writes=["Ev%d" % b], key="Ev%d" % b)
                    P.dma("sync", gh[b][:], gaT[h * 128:(h + 1) * 128, :], writes=["Eg%d" % b], key="Eg%d" % b)

                sti = [0]
                pti = [0]
                oi = [0]
                yi = [0]
                loadE(0)
                P.dma("sync", bAf[:], biasA[0], writes=["bAf"], key="Ebf")
                for h in range(8):
                    b = h % 2
                    Q, K, V, G, O = qh[b], kh[b], vh[b], gh[b], oh[b]
                    qk, kk, vk, gk, ok = "Eq%d" % b, "Ek%d" % b, "Ev%d" % b, "Eg%d" % b, "Eo%d" % b
                    P.op("dve", TT(bA[:], bAf[:], mAf[:], ALU.add), reads=["bAf", "mAf"], writes=["bA"])
                    if h + 1 < 8:
                        loadE(h + 1)
                        P.dma("sync", bAf[:], biasA[h + 1], writes=["bAf"], key="Ebf")
                    P.op("dve", lambda e, K=K: e.tensor_reduce(out=ksum[:, 0:NBLK], in_=K[:].rearrange("p (n l) -> p n l", l=256),
                                                              axis=AX.X, op=ALU.add),
                         reads=[kk], writes=["ksum"])
                    P.op("act", ACT(kmb[:], ksum[:], AF.Copy, scale=1.0 / 256), reads=["ksum"], writes=["kmb"])
                    for i in range(NT128):
                        P.op("pe", MM(gps[:, i * 16:(i + 1) * 16], Q[:, i * 128:(i + 1) * 128], kmb[:], True, True),
                             reads=[qk, "kmb"], writes=["Est2"], signal=(i == NT128 - 1))
                    P.op("dve", TT(gm[:], gps[:, 0:NT128 * 16], pms[:], ALU.add), reads=["Est2", "pms"], writes=["gm"])
                    for i in range(NT128):
                        P.op("dve", lambda e, i=i: e.max(out=top8[:, i, :], in_=gm[:, i * 16:(i + 1) * 16]),
                             reads=["gm"], writes=["t8_%d" % i])
                        P.op("dve", TS(msel[:, i, :], gm[:, i * 16:(i + 1) * 16], top8[:, i, 3:4], 1.0, ALU.is_ge, ALU.subtract),
                             reads=["gm", "t8_%d" % i], writes=["ms_%d" % i])
                    for qt in range(NT512):
                        for k4 in range(4):
                            i = qt * 4 + k4
                            P.op("pe", TR(trp[0:16, k4 * 128:(k4 + 1) * 128], msel[:, i, :], idf32[:]),
                                 reads=["ms_%d" % i, "idf32"], writes=["trp"], signal=(k4 == 3))
                        P.op("act", ACT(mselT[0:16, qt * 512:(qt + 1) * 512], trp[0:16, 0:512], AF.Copy),
                             reads=["trp"], writes=["mselT"])
                    tiles = [(qt, kc) for qt in range(NT512) for kc in range(4 * (qt + 1))]

                    def emit_st(qt, kc):
                        qs = slice(qt * 512, (qt + 1) * 512)
                        n = kc // 2
                        jrel = kc - (4 * qt - 2)
                        near = 0 <= jrel < 6
                        s_ = sti[0] % 3
                        sti[0] += 1
                        stk = "Est%d" % s_
                        P.op("pe", MM(stp[s_][:, :], K[:, kc * 128:(kc + 1) * 128], Q[:, qs], True, False),
                             reads=[kk, qk], writes=[stk], signal=False)
                        P.op("pe", MM(stp[s_][:, :], embf[:, n * 128:(n + 1) * 128], mselT[:, qs], False, not near),
                             reads=["embf", "mselT"], writes=[stk], signal=(not near))
                        if near:
                            P.op("pe", MM(stp[s_][:, :], ident[:], bA[:, jrel, :], False, True),
                                 reads=["ident", "bA"], writes=[stk])
                        p_ = pti[0] % 4
                        pti[0] += 1
                        ptk = "Ept%d" % p_
                        if near:
                            P.op("act", ACT(pt[p_][:], stp[s_][:, :], AF.Exp), reads=[stk], writes=[ptk])
                        else:
                            P.op("act", ACT(pt[p_][:], stp[s_][:, :], AF.Exp, bias=c31s[:, h:h + 1]),
                                 reads=[stk, "c31"], writes=[ptk])
                        return p_

                    def emit_pv(qt, kc, p_, ob_):
                        nkc = 4 * (qt + 1)
                        qs = slice(qt * 512, (qt + 1) * 512)
                        ptk = "Ept%d" % p_
                        last = kc == nkc - 1
                        for sub in range(4):
                            bank = acc[ob_][sub // 2]
                            c0 = (sub % 2) * 129
                            bkey = "Eacc%d_%d" % (ob_, sub // 2)
                            P.op("pe", (lambda e, bank=bank, c0=c0, sub=sub, p_=p_, kc=kc, last=last, V=V:
                                        e.matmul(bank[:, c0:c0 + 129], pt[p_][:, sub * 128:(sub + 1) * 128], V[:, kc, :],
                                                 start=(kc == 0 and sub % 2 == 0), stop=last, skip_group_check=True)),
                                 reads=[vk, ptk], writes=[bkey], signal=(sub == 3))
                        if last:
                            for sub in range(4):
                                bank = acc[ob_][sub // 2]
                                c0 = (sub % 2) * 129
                                bkey = "Eacc%d_%d" % (ob_, sub // 2)
                                y_ = sub
                                P.op("dve", lambda e, bank=bank, c0=c0, sub=sub: e.reciprocal(out=rdn[:, sub:sub + 1], in_=bank[:, c0 + 128:c0 + 129]),
                                     reads=[bkey], writes=["rdn%d" % sub])
                                P.op("dve", TS(yn[y_][:], bank[:, c0:c0 + 128], rdn[:, sub:sub + 1], None, ALU.mult),
                                     reads=[bkey, "rdn%d" % sub], writes=["Eyn%d" % y_])

                    def emit_ep(qt):
                        qs = slice(qt * 512, (qt + 1) * 512)
                        for sub in range(4):
                            P.op("pe", TR(trp[:, sub * 128:(sub + 1) * 128], yn[sub][:], idf32[:]),
                                 reads=["Eyn%d" % sub, "idf32"], writes=["trp"], signal=(sub == 3))
                        P.op("dve", TT(O[:, qs], trp[:, :], G[:, qs], ALU.mult), reads=["trp", gk], writes=[ok])

                    pq_ = [emit_st(*tiles[0])]
                    if len(tiles) > 1:
                        pq_.append(emit_st(*tiles[1]))
                    pend = None
                    for ti, (qt, kc) in enumerate(tiles):
                        if kc == 0:
                            oi[0] += 1
                        if ti + 2 < len(tiles):
                            pq_.append(emit_st(*tiles[ti + 2]))
                        pcur = pq_.pop(0)
                        emit_pv(qt, kc, pcur, oi[0] % 2)
                        if pend is not None:
                            emit_ep(pend)
                            pend = None
                        if kc == 4 * (qt + 1) - 1:
                            pend = qt
                    if pend is not None:
                        emit_ep(pend)
                    P.dma("sync", ygT[h * 128:(h + 1) * 128, :], O[:], reads=[ok], key=ok)
                P.flush()

        def phase_F1(l):
            with ExitStack() as es:
                wb = sb(es, nc, "F_wb", [128, 24, D], BF16)
                yg = [sb(es, nc, f"F_yg{i}", [128, 24, 512], BF16) for i in range(2)]
                gt = [sb(es, nc, f"F_gt{i}", [128, 3, 512], BF16) for i in range(2)]
                m = [sb(es, nc, f"F_m{i}", [128, 512], F32) for i in range(3)]
                mo = sb(es, nc, "F_mo", [128, 16, 512], BF16)
                pp = [[ps(es, nc, f"F_p{i}_{n}", [128, 512], F32) for n in range(3)] for i in range(2)]
                for n in range(3):
                    for hh in range(2):
                        P.dma("pool", wb[:, n * 8 + hh * 4:n * 8 + hh * 4 + 4, :],
                              w_br[l, n, hh * 512:(hh + 1) * 512, :].rearrange("(cc q) d -> q cc d", q=128),
                              writes=["wb%d" % (n * 2 + hh)], key="Fwb%d" % (n * 2 + hh))
                it = 0
                def loadYG(tt):
                    yb = tt % 2
                    P.dma("sync", yg[yb][:], ygT[:, tt * 512:(tt + 1) * 512].rearrange("(c q) t -> q c t", q=128),
                          writes=["Fyg%d" % yb], key="Fyg%d" % yb)
                def loadGT(it_):
                    tt_, j_ = divmod(it_, 16)
                    b_ = it_ % 2
                    P.dma("sync", gt[b_][:],
                          mgT[:, tt_ * 512:(tt_ + 1) * 512].rearrange("(n r) t -> r n t", n=3)[j_ * 128:(j_ + 1) * 128],
                          writes=["Fgt%d" % b_], key="Fgt%d" % b_)
                loadYG(0)
                loadGT(0)
                for tt in range(NT512):
                    ts = slice(tt * 512, (tt + 1) * 512)
                    yb = tt % 2
                    if tt + 1 < NT512:
                        loadYG(tt + 1)
                    for j in range(16):
                        b = it % 2
                        it += 1
                        if it < NT512 * 16:
                            loadGT(it)
                        for n in range(3):
                            for cc in range(8):
                                P.op("pe", MM(pp[b][n][:, :], wb[:, n * 8 + cc, j * 128:(j + 1) * 128], yg[yb][:, n * 8 + cc, :],
                                              cc == 0, cc == 7),
                                     reads=["wb%d" % (n * 2 + cc // 4), "Fyg%d" % yb], writes=["Fp%d_%d" % (b, n)], signal=(cc == 7))
                            P.op("dve", TT(m[n][:], pp[b][n][:, :], gt[b][:, n, :], ALU.mult),
                                 reads=["Fp%d_%d" % (b, n), "Fgt%d" % b], writes=["Fm%d" % n])
                        P.op("pool", TT(m[0][:], m[0][:], m[1][:], ALU.add), reads=["Fm0", "Fm1"], writes=["Fm0"])
                        P.op("pool", TT(mo[:, j, :], m[0][:], m[2][:], ALU.add), reads=["Fm0", "Fm2"], writes=["Fmo"])
                    P.dma("sync", mrgT[:, ts].rearrange("(j q) t -> q j t", q=128), mo[:], reads=["Fmo"], key="Fmo")
                P.flush()

        def load_G_weights(l, wg, wp):
            for kq in range(4):
                P.dma("pool", wg[:, kq * 4:(kq + 1) * 4, :],
                      w_pg[l, kq * 512:(kq + 1) * 512, :].rearrange("(kc q) d -> q kc d", q=128),
                      writes=["wg"], key="Hwg%d" % kq)
            P.dma("pool", wp[:], w_pp[l].rearrange("(kc q) d -> q kc d", q=128), writes=["wp"], key="Hwp")

        def phase_F2(l, xsrc, wg, wp):
            with ExitStack() as es:
                wo = sb(es, nc, "G_wo", [128, 16, D], BF16)
                mt = [sb(es, nc, f"G_mt{i}", [128, 16, 512], BF16) for i in range(2)]
                xi = [sb(es, nc, f"G_xi{i}", [128, D], F32) for i in range(2)]
                xo = [sb(es, nc, f"G_xo{i}", [128, D], F32) for i in range(2)]
                pq = [ps(es, nc, f"G_p{i}", [128, 512], F32) for i in range(4)]
                for kq in range(4):
                    P.dma("pool", wo[:, kq * 4:(kq + 1) * 4, :],
                          w_out[l, kq * 512:(kq + 1) * 512, :].rearrange("(kc q) d -> q kc d", q=128),
                          writes=["wo%d" % kq], key="Gwo%d" % kq)
                load_G_weights(l, wg, wp)
                pk = 0
                def loadMT(tt_):
                    mb_ = tt_ % 2
                    P.dma("sync", mt[mb_][:], mrgT[:, tt_ * 512:(tt_ + 1) * 512].rearrange("(kc q) t -> q kc t", q=128),
                          writes=["Gmt%d" % mb_], key="Gmt%d" % mb_)
                def loadXI(i_):
                    b_ = i_ % 2
                    P.dma("sync", xi[b_][:], xsrc[i_ * 128:(i_ + 1) * 128, :], writes=["Gxi%d" % b_], key="Gxi%d" % b_)
                loadMT(0)
                loadXI(0)
                for tt in range(NT512):
                    mb = tt % 2
                    if tt + 1 < NT512:
                        loadMT(tt + 1)
                    for k4 in range(4):
                        i = tt * 4 + k4
                        b = i % 2
                        if i + 1 < NT128:
                            loadXI(i + 1)
                        for dt_ in range(4):
                            pb_ = pk % 4
                            pk += 1
                            for kc in range(16):
                                P.op("pe", MM(pq[pb_][:, :], mt[mb][:, kc, k4 * 128:(k4 + 1) * 128], wo[:, kc, dt_ * 512:(dt_ + 1) * 512],
                                              kc == 0, kc == 15),
                                     reads=["Gmt%d" % mb, "wo%d" % (kc // 4)], writes=["Gp%d" % pb_], signal=(kc == 15))
                            P.op("dve", TT(xo[b][:, dt_ * 512:(dt_ + 1) * 512], pq[pb_][:, :], xi[b][:, dt_ * 512:(dt_ + 1) * 512], ALU.add),
                                 reads=["Gp%d" % pb_, "Gxi%d" % b], writes=["Gxo%d" % b])
                        P.dma("sync", xm[i * 128:(i + 1) * 128, :], xo[b][:], reads=["Gxo%d" % b], key="Gxo%d" % b)
                P.flush()

        def phase_G(l, xdst, wg, wp, last=False):
            with ExitStack() as es:
                gbc = sb(es, nc, "H_g", [128, D], F32)
                if last:
                    fgb = sb(es, nc, "H_fg", [128, D], F32)
                    yo = [sb(es, nc, f"H_yo{i}", [128, D], F32) for i in range(2)]
                    fss = sb(es, nc, "H_fss", [128, NT128], F32)
                    frs = sb(es, nc, "H_frs", [128, NT128], F32)
                    P.dma("sync", fgb[:], fin_g.partition_broadcast(128), writes=["Hfg"], key="Hfg")
                xin = [sb(es, nc, f"H_x{i}", [128, D], F32) for i in range(4)]
                pin = [sb(es, nc, f"H_p{i}", [128, PLE], F32) for i in range(4)]
                pbf = [sb(es, nc, f"H_pb{i}", [128, PLE], BF16) for i in range(2)]
                hb = [sb(es, nc, f"H_hb{i}", [128, D], BF16) for i in range(2)]
                jk = sb(es, nc, "H_jk", [128, D], BF16)
                ssum = sb(es, nc, "H_ss", [128, NT128], F32)
                sq = sb(es, nc, "H_sq", [128, NT128], F32)
                rsd = sb(es, nc, "H_rs", [128, NT128], F32)
                hT2 = [sb(es, nc, f"H_hT{i}", [128, 16, 128], BF16) for i in range(2)]
                pT2 = [sb(es, nc, f"H_pT{i}", [128, 2, 128], BF16) for i in range(2)]
                sg2 = [sb(es, nc, f"H_sg{i}", [128, 512], F32) for i in range(2)]
                tq2 = [sb(es, nc, f"H_tq{i}", [128, 512], F32) for i in range(2)]
                xo = [sb(es, nc, f"H_xo{i}", [128, D], F32) for i in range(2)]
                ptr = [ps(es, nc, f"H_pt{i}", [128, 1024], BF16) for i in range(2)]
                ptp = ps(es, nc, "H_ptp", [128, 1024], BF16)
                pg = [ps(es, nc, f"H_pg{i}", [128, 512], F32) for i in range(2)]
                ppp = [ps(es, nc, f"H_pp{i}", [128, 512], F32) for i in range(2)]
                P.dma("sync", gbc[:], ple_g[l].partition_broadcast(128), writes=["Hg"], key="Hg")
                pk = [0]

                def loadH(i_):
                    b_ = i_ % 4
                    P.dma("sync", xin[b_][:], xm[i_ * 128:(i_ + 1) * 128, :], writes=["Hx%d" % b_], key="Hx%d" % b_)
                    P.dma("sync", pin[b_][:], p[l, i_ * 128:(i_ + 1) * 128, :], writes=["Hp%d" % b_], key="Hp%d" % b_)

                def normH(i):
                    b = i % 2
                    x3 = i % 4
                    P.op("pool", CP(pbf[b][:], pin[x3][:]), reads=["Hp%d" % x3], writes=["Hpbf%d" % b])
                    norm_tile_eps(P, "H", xin[x3], gbc, hb[b], ssum, sq, rsd, jk, i, xkey="Hx%d" % x3)

                def prepH(i):
                    b = i % 2
                    x3 = i % 4
                    hT = hT2[b]
                    pT = pT2[b]
                    for half in range(2):
                        for j in range(8):
                            kc = half * 8 + j
                            P.op("pe", TR(ptr[half][:, j * 128:(j + 1) * 128], hb[b][:, kc * 128:(kc + 1) * 128], ident[:]),
                                 reads=["Hhb%d" % b, "ident"], writes=["Hptr%d" % half], signal=(j == 7))
                        P.op("act", ACT(hT[:, half * 8:(half + 1) * 8, :], ptr[half][:].rearrange("p (j t) -> p j t", t=128), AF.Copy),
                             reads=["Hptr%d" % half], writes=["HhT%d" % b])
                    for j in range(2):
                        P.op("pe", TR(ptp[:, j * 128:(j + 1) * 128], pbf[b][:, j * 128:(j + 1) * 128], ident[:]),
                             reads=["Hpbf%d" % b, "ident"], writes=["Hptp"], signal=(j == 1))
                    P.op("act", ACT(pT[:], ptp[:, 0:256].rearrange("p (j t) -> p j t", t=128), AF.Copy),
                         reads=["Hptp"], writes=["HpT%d" % b])

                def mainH(i):
                    b = i % 2
                    x3 = i % 4
                    hT = hT2[b]
                    pT = pT2[b]
                    for dt_ in range(4):
                        ds_ = slice(dt_ * 512, (dt_ + 1) * 512)
                        k = pk[0] % 2
                        pk[0] += 1
                        for kc in range(16):
                            P.op("pe", MM(pg[k][:, :], hT[:, kc, :], wg[:, kc, ds_], kc == 0, kc == 15),
                                 reads=["HhT%d" % b, "wg"], writes=["Hpg%d" % k], signal=(kc == 15))
                        for kc in range(2):
                            P.op("pe", MM(ppp[k][:, :], pT[:, kc, :], wp[:, kc, ds_], kc == 0, kc == 1),
                                 reads=["HpT%d" % b, "wp"], writes=["Hpp%d" % k], signal=(kc == 1))
                        sg = sg2[k]
                        tq = tq2[k]
                        P.op("act", ACT(sg[:], pg[k][:, :], AF.Sigmoid), reads=["Hpg%d" % k], writes=["Hsg%d" % k])
                        P.op("dve", TT(tq[:], ppp[k][:, :], sg[:], ALU.mult), reads=["Hpp%d" % k, "Hsg%d" % k], writes=["Htq%d" % k])
                        P.op("pool", TT(xo[b][:, ds_], tq[:], xin[x3][:, ds_], ALU.add), reads=["Htq%d" % k, "Hx%d" % x3], writes=["Hxo%d" % b])
                    if not last:
                        P.dma("sync", xdst[i * 128:(i + 1) * 128, :], xo[b][:], reads=["Hxo%d" % b], key="Hxo%d" % b)
                    else:
                        P.op("dve", STT(jk[:], xo[b][:], 1.0, xo[b][:], ALU.mult, ALU.mult, accum_out=fss[:, i:i + 1]),
                             reads=["Hxo%d" % b], writes=["Hjk", "Hfss%d" % i])
                        P.op("dve", TS(fss[:, i:i + 1], fss[:, i:i + 1], 1.0 / D, EPS, ALU.mult, ALU.add),
                             reads=["Hfss%d" % i], writes=["Hfss%d" % i])
                        P.op("pool", TT(frs[:, i:i + 1], fss[:, i:i + 1], mhalf[:, 0:1], ALU.pow),
                             reads=["Hfss%d" % i, "mhalf"], writes=["Hfrs%d" % i])
                        P.op("dve", STT(yo[b][:], xo[b][:], frs[:, i:i + 1], fgb[:], ALU.mult, ALU.mult),
                             reads=["Hxo%d" % b, "Hfrs%d" % i, "Hfg"], writes=["Hyo%d" % b])
                        P.dma("sync", y[i * 128:(i + 1) * 128, :], yo[b][:], reads=["Hyo%d" % b], key="Hyo%d" % b)

                hb3 = hb
                for i0 in range(min(3, NT128)):
                    loadH(i0)
                normH(0)
                if NT128 > 1:
                    normH(1)
                prepH(0)
                for i in range(NT128):
                    if i + 3 < NT128:
                        loadH(i + 3)
                    if i + 1 < NT128:
                        prepH(i + 1)
                    if i + 2 < NT128:
                        normH(i + 2)
                    mainH(i)
                P.flush()

        def phase_final(xsrc):
            with ExitStack() as es:
                xin = [sb(es, nc, f"Z_x{i}", [128, D], F32) for i in range(2)]
                yo = [sb(es, nc, f"Z_y{i}", [128, D], F32) for i in range(2)]
                jk = sb(es, nc, "Z_jk", [128, D], BF16)
                gbc = sb(es, nc, "Z_g", [128, D], F32)
                ssum = sb(es, nc, "Z_ss", [128, NT128], F32)
                sq = sb(es, nc, "Z_sq", [128, NT128], F32)
                rsd = sb(es, nc, "Z_rs", [128, NT128], F32)
                P.dma("sync", gbc[:], fin_g.partition_broadcast(128), writes=["Zg"], key="Zg")
                def loadZ(i_):
                    b_ = i_ % 2
                    P.dma("sync", xin[b_][:], xsrc[i_ * 128:(i_ + 1) * 128, :], writes=["Zx%d" % b_], key="Zx%d" % b_)
                loadZ(0)
                for i in range(NT128):
                    b = i % 2
                    if i + 1 < NT128:
                        loadZ(i + 1)
                    P.op("dve", STT(jk[:], xin[b][:], 1.0, xin[b][:], ALU.mult, ALU.mult, accum_out=ssum[:, i:i + 1]),
                         reads=["Zx%d" % b], writes=["Zjk", "Zss%d" % i])
                    P.op("dve", TS(ssum[:, i:i + 1], ssum[:, i:i + 1], 1.0 / D, EPS, ALU.mult, ALU.add),
                         reads=["Zss%d" % i], writes=["Zss%d" % i])
                    P.op("pool", TT(rsd[:, i:i + 1], ssum[:, i:i + 1], mhalf[:, 0:1], ALU.pow),
                         reads=["Zss%d" % i, "mhalf"], writes=["Zrs%d" % i])
                    P.op("dve", STT(yo[b][:], xin[b][:], rsd[:, i:i + 1], gbc[:], ALU.mult, ALU.mult),
                         reads=["Zx%d" % b, "Zrs%d" % i, "Zg"], writes=["Zy%d" % b])
                    P.dma("sync", y[i * 128:(i + 1) * 128, :], yo[b][:], reads=["Zy%d" % b], key="Zy%d" % b)
                P.flush()

        xcur = x
        for l in range(NL):
            with ExitStack() as esb:
                hT = sb(esb, nc, "hT", [128, 16, T], BF16)
                phase_A(l, xcur, hT)
                phase_B(l, hT)
            phase_C(l)
            phase_D(l)
            phase_E(l)
            phase_F1(l)
            with ExitStack() as esw:
                wg_ = sb(esw, nc, "H_wg", [128, 16, D], BF16)
                wp_ = sb(esw, nc, "H_wp", [128, 2, D], BF16)
                phase_F2(l, xcur, wg_, wp_)
                phase_G(l, xs[l], wg_, wp_, last=(l == NL - 1))
            xcur = xs[l]
    return nc


def _bucket(dist):
    n = np.maximum(dist, 0)
    max_exact = 16
    large = max_exact + (np.log(np.maximum(n, 1).astype(np.float32) / max_exact)
                         / math.log(128 / max_exact) * (32 - max_exact)).astype(np.int32)
    large = np.minimum(large, 31)
    return np.where(n < max_exact, n, large)


def host_constants(T, rpe_table):
    NT128 = T // 128
    c = {}
    c["identf"] = np.eye(128, dtype=np.float32)
    em = np.zeros((16, 16, 128), np.float32)
    for n in range(16):
        em[n, n, :] = 30000.0
    c["emat"] = em.reshape(16, 16 * 128)
    pm = np.full((128, NT128, 16), -BIG, np.float32)
    for i in range(NT128):
        qb = i // 2
        pm[:, i, :qb] = 0.0
        pm[:, i, qb] = BIG
    c["pastm"] = pm.reshape(128, NT128 * 16)
    ki = np.arange(128)[:, None, None]
    jj = np.arange(6)[None, :, None]
    qi = np.arange(512)[None, None, :]
    dist = qi + 256 - jj * 128 - ki
    bk = _bucket(dist)
    tabA = rpe_table[:, :8]
    c["biasA"] = np.ascontiguousarray(np.transpose(tabA[bk], (3, 0, 1, 2))).astype(np.float32)
    c["maskA"] = np.where(dist >= 0, 0.0, NEGM).astype(np.float32)
    c["c31"] = np.ascontiguousarray(np.broadcast_to(tabA[31][None, :], (128, 8))).astype(np.float32)
    ki = np.arange(128)[:, None]
    qi = np.arange(128)[None, :]
    d_prev = qi + 128 - ki
    d_own = qi - ki
    tabB = rpe_table[:, 8:]
    bB = np.stack([np.transpose(tabB[_bucket(d_prev)], (0, 2, 1)),
                   np.transpose(tabB[_bucket(d_own)], (0, 2, 1))], axis=1)
    c["biasB"] = np.ascontiguousarray(bB).astype(np.float32)
    mB = np.stack([np.where((d_prev >= 0) & (d_prev < 128), 0.0, NEGM),
                   np.where((d_own >= 0) & (d_own < 128), 0.0, NEGM)], axis=1)
    c["maskB"] = np.ascontiguousarray(np.broadcast_to(mB[:, :, None, :], (128, 2, 16, 128))).astype(np.float32)
    return c


def host_cvec(conv_w, conv_b, b_r, b_i, lam):
    NL = conv_w.shape[0]
    f = np.concatenate([conv_w, conv_b[:, None], b_r[:, None], b_i[:, None], lam[:, None]], axis=1)
    f = f.reshape(NL, 8, 8, 128)
    return np.ascontiguousarray(np.transpose(f, (0, 3, 2, 1))).astype(np.float32)


_NC_CACHE = {}


def kernel(x, p, rpe_table, norm_g, w_in, sinks, conv_w, conv_b, w_r, b_r, w_i, b_i, lam,
           w_br, w_out, ple_norm_g, w_pg, w_pp, final_norm_g):
    x = np.asarray(x, np.float32)
    B, T, _ = x.shape
    NL = w_in.shape[0]
    key = (T, NL)
    if key not in _NC_CACHE:
        _NC_CACHE[key] = build_program(T, NL)
    nc = _NC_CACHE[key]
    consts = host_constants(T, np.asarray(rpe_table, np.float32))
    shared = dict(
        norm_g=np.asarray(norm_g, np.float32), w_in=np.asarray(w_in, np.float32),
        sinks=np.asarray(sinks, np.float32),
        cvec=host_cvec(np.asarray(conv_w, np.float32), np.asarray(conv_b, np.float32), np.asarray(b_r, np.float32),
                       np.asarray(b_i, np.float32), np.asarray(lam, np.float32)),
        w_r=np.asarray(w_r, np.float32), w_i=np.asarray(w_i, np.float32), w_br=np.asarray(w_br, np.float32),
        w_out=np.asarray(w_out, np.float32), ple_norm_g=np.asarray(ple_norm_g, np.float32),
        w_pg=np.asarray(w_pg, np.float32), w_pp=np.asarray(w_pp, np.float32),
        final_norm_g=np.asarray(final_norm_g, np.float32), **consts)
    n_cores = 8
    active = [0, 1, 4, 5][:B]
    parr = np.asarray(p, np.float32)
    zeros = {k: (v if k in consts else np.zeros_like(v)) for k, v in shared.items()}
    zx = np.zeros_like(x[0])
    zp = np.zeros_like(parr[:, 0])
    in_maps = []
    for c in range(n_cores):
        if c in active:
            b = active.index(c)
            m = dict(shared)
            m["x"] = np.ascontiguousarray(x[b])
            m["p"] = np.ascontiguousarray(parr[:, b])
        else:
            m = dict(zeros)
            m["x"] = zx
            m["p"] = zp
        in_maps.append(m)
    res = run_bass_kernel_spmd(nc, in_maps, core_ids=list(range(n_cores)))
    return np.stack([np.asarray(res.results[c]["y"], np.float32) for c in active], axis=0)
```

```python
import math
from contextlib import ExitStack

import numpy as np
import concourse.bass as bass
import concourse.mybir as mybir
from concourse.bass_utils import run_bass_kernel_spmd

F32 = mybir.dt.float32
BF16 = mybir.dt.bfloat16
AF = mybir.ActivationFunctionType
ALU = mybir.AluOpType
AX = mybir.AxisListType

D = 2048
INW = 14848
PLE = 256
EPS = 1e-6
OFF = dict(qa=0, ka=1024, va=2048, ga=3072, qb=4096, kb=5120, vb=5376, gb=5632,
           xc=6656, gc=7680, mg=8704)
NEGM = -30000.0
BIG = 1.0e30
ENGS = ("sync", "act", "dve", "pool", "pe")


class Prog:
    def __init__(self, nc, es, n_dma_sems=52):
        self.nc = nc
        self.semh = {}
        for e in ("act", "dve", "pool", "pe"):
            self.semh["c_" + e] = es.enter_context(nc.semaphore("c_" + e))
        self.all_dma = []
        for i in range(n_dma_sems):
            self.semh[f"dq{i}"] = es.enter_context(nc.semaphore(f"dq{i}"))
            self.all_dma.append(f"dq{i}")
        self.all_sw = []
        for i in range(12):
            self.semh[f"sq{i}"] = es.enter_context(nc.semaphore(f"sq{i}"))
            self.all_sw.append(f"sq{i}")
        self.free_sw = list(self.all_sw)
        self.free_dma = list(self.all_dma)
        self.key2sem = {}
        self.dma_cnt = {s: 0 for s in self.all_dma + self.all_sw}
        self.cnt = {e: 0 for e in ENGS}
        self.unsig = {e: False for e in ENGS}
        self.seen = {e: {} for e in ENGS}
        self.ops = {e: [] for e in ENGS}
        self.state = {}

    def _st(self, b):
        s = self.state.get(b)
        if s is None:
            s = self.state[b] = {"w": None, "r": []}
        return s

    def _deps(self, eng, reads, writes):
        deps = {}
        def add(tok):
            if tok is None:
                return
            s, v = tok
            if eng == "pe" and s == "c_pe":
                return
            if deps.get(s, 0) < v:
                deps[s] = v
        for b in reads:
            add(self._st(b)["w"])
        for b in writes:
            st = self._st(b)
            add(st["w"])
            for t in st["r"]:
                add(t)
        out = []
        for s, v in deps.items():
            if self.seen[eng].get(s, 0) < v:
                self.seen[eng][s] = v
                out.append((s, v))
        return out

    def _commit(self, tok, reads, writes):
        for b in reads:
            self._st(b)["r"].append(tok)
        for b in writes:
            st = self._st(b)
            st["w"] = tok
            st["r"] = []

    def op(self, eng, fn, reads=(), writes=(), signal=True):
        waits = self._deps(eng, reads, writes)
        s = "c_" + eng
        if signal:
            self.cnt[eng] += 1
            tok = (s, self.cnt[eng])
            self.unsig[eng] = False
            inc = (s, 1)
        else:
            tok = (s, self.cnt[eng] + 1)
            self.unsig[eng] = True
            inc = None
        self.ops[eng].append((waits, fn, inc))
        self._commit(tok, reads, writes)

    def dma(self, q, out, in_, reads=(), writes=(), key=None):
        assert key is not None
        waits = self._deps(q, reads, writes)
        if key not in self.key2sem:
            self.key2sem[key] = self.free_sw.pop() if q == "pool" else self.free_dma.pop()
        s = self.key2sem[key]
        self.dma_cnt[s] += 16
        tok = (s, self.dma_cnt[s])
        self.ops[q].append((waits, (lambda e, o=out, i=in_: e.dma_start(out=o, in_=i)), (s, 16)))
        self._commit(tok, reads, writes)

    def barrier(self):
        for e in ENGS:
            assert not self.unsig[e], e
        for e in ENGS:
            waits = []
            for f in ("act", "dve", "pool", "pe"):
                s = "c_" + f
                if f != e and self.cnt[f] > self.seen[e].get(s, 0):
                    self.seen[e][s] = self.cnt[f]
                    waits.append((s, self.cnt[f]))
            for s, v in self.dma_cnt.items():
                if v > self.seen[e].get(s, 0):
                    self.seen[e][s] = v
                    waits.append((s, v))
            if waits:
                self.ops[e].append((waits, None, None))
        self.state = {}
        self.key2sem = {}
        self.free_dma = list(self.all_dma)
        self.free_sw = list(self.all_sw)

    def flush(self):
        self.barrier()
        nc = self.nc
        semh = self.semh
        with nc.Block() as block:
            decos = dict(sync=block.sync, act=block.scalar, dve=block.vector,
                         pool=block.gpsimd, pe=block.tensor)
            for name in ENGS:
                ops = self.ops[name]
                if not ops:
                    continue
                def body(eng, ops=ops):
                    for waits, fn, inc in ops:
                        for s, v in waits:
                            eng.wait_ge(semh[s], v)
                        if fn is not None:
                            ins = fn(eng)
                            if inc is not None:
                                ins.then_inc(semh[inc[0]], inc[1])
                decos[name](body)
        self.ops = {e: [] for e in ENGS}


_UID = [0]


def _uniq(name):
    _UID[0] += 1
    return "%s_%d" % (name, _UID[0])


def sb(es, nc, name, shape, dt):
    return es.enter_context(nc.sbuf_tensor(_uniq(name), list(shape), dt))


def ps(es, nc, name, shape, dt):
    return es.enter_context(nc.psum_tensor(_uniq(name), list(shape), dt))


def MM(out, lhsT, rhs, start, stop):
    return lambda e: e.matmul(out, lhsT, rhs, start=start, stop=stop)


def TR(out, in_, ident):
    return lambda e: e.transpose(out, in_, ident)


def ACT(out, in_, func, bias=None, scale=None, accum_out=None):
    kw = {}
    if bias is not None:
        kw["bias"] = bias
    if scale is not None:
        kw["scale"] = scale
    if accum_out is not None:
        kw["accum_out"] = accum_out
    return lambda e: e.activation(out=out, in_=in_, func=func, **kw)


def TT(out, in0, in1, op):
    return lambda e: e.tensor_tensor(out=out, in0=in0, in1=in1, op=op)


def TS(out, in0, s1, s2, op0, op1=None):
    if op1 is None:
        return lambda e: e.tensor_scalar(out=out, in0=in0, scalar1=s1, scalar2=None, op0=op0)
    return lambda e: e.tensor_scalar(out=out, in0=in0, scalar1=s1, scalar2=s2, op0=op0, op1=op1)


def STT(out, in0, scalar, in1, op0, op1, accum_out=None):
    if accum_out is None:
        return lambda e: e.scalar_tensor_tensor(out=out, in0=in0, scalar=scalar, in1=in1, op0=op0, op1=op1)
    return lambda e: e.scalar_tensor_tensor(out=out, in0=in0, scalar=scalar, in1=in1, op0=op0, op1=op1,
                                            accum_out=accum_out)


def CP(out, in_):
    return lambda e: e.tensor_copy(out=out, in_=in_)


class Ctx:
    pass


def build_program(T=4096, NL=2, debug=False):
    assert T % 512 == 0 and T // 256 <= 16
    nc = bass.Bass("TRN2", target_bir_lowering=False)
    C = Ctx()
    C.T = T
    NT128 = T // 128
    NT512 = T // 512
    NBLK = T // 256

    def din(name, shape, dt=F32):
        return nc.dram_tensor(name, list(shape), dt, kind="ExternalInput").ap()

    x = din("x", [T, D])
    p = din("p", [NL, T, PLE])
    norm_g = din("norm_g", [NL, D])
    w_in = din("w_in", [NL, D, INW])
    sinks = din("sinks", [NL, 16])
    cvec = din("cvec", [NL, 128, 8, 8])
    w_r = din("w_r", [NL, 8, 128, 128])
    w_i = din("w_i", [NL, 8, 128, 128])
    w_br = din("w_br", [NL, 3, 1024, D])
    w_out = din("w_out", [NL, D, D])
    ple_g = din("ple_norm_g", [NL, D])
    w_pg = din("w_pg", [NL, D, D])
    w_pp = din("w_pp", [NL, PLE, D])
    fin_g = din("final_norm_g", [D])
    identf = din("identf", [128, 128])
    emat = din("emat", [16, 16 * 128])
    pastm = din("pastm", [128, NT128 * 16])
    biasA = din("biasA", [8, 128, 6, 512])
    maskA = din("maskA", [128, 6, 512])
    c31 = din("c31", [128, 8])
    biasB = din("biasB", [128, 2, 16, 128])
    maskB = din("maskB", [128, 2, 16, 128])
    y = nc.dram_tensor("y", [T, D], F32, kind="ExternalOutput").ap()

    skind = "ExternalOutput" if debug else "Internal"

    def dscr(name, shape, dt):
        return nc.dram_tensor(name, list(shape), dt, kind=skind).ap()

    qaT = dscr("qaT", [1024, T], BF16)
    kaT = dscr("kaT", [1024, T], BF16)
    va = dscr("va", [T, 1024], BF16)
    gaT = dscr("gaT", [1024, T], BF16)
    qbT = dscr("qbT", [1024, T], BF16)
    kbT = dscr("kbT", [256, T], BF16)
    vb = dscr("vb", [T, 256], BF16)
    gbT = dscr("gbT", [1024, T], BF16)
    xcT = dscr("xcT", [1024, T], F32)
    gcT = dscr("gcT", [1024, T], BF16)
    mgT = dscr("mgT", [3 * D, T], BF16)
    ygT = dscr("ygT", [3 * 1024, T], BF16)
    mrgT = dscr("mrgT", [D, T], BF16)
    xm = dscr("xm", [T, D], F32)
    xs = [dscr(f"xs{l}", [T, D], F32) for l in range(NL)]

    with ExitStack() as top:
        P = Prog(nc, top)
        ident = sb(top, nc, "ident", [128, 128], BF16)
        ones = sb(top, nc, "ones", [128, 128], BF16)
        embf = sb(top, nc, "embf", [128, 16 * 128], BF16)
        mhalf = sb(top, nc, "mhalf", [128, 1], F32)

        with ExitStack() as es:
            idf = sb(es, nc, "S_idf", [128, 128], F32)
            emf = sb(es, nc, "S_emf", [16, 16 * 128], F32)
            P.dma("sync", idf[:], identf[:, :], writes=["idf"], key="s0")
            P.dma("sync", emf[:], emat[:, :], writes=["emf"], key="s1")
            P.op("dve", CP(ident[:], idf[:]), reads=["idf"], writes=["ident"])
            P.op("pool", lambda e: e.memset(embf[:], 0.0), writes=["embf"])
            P.op("dve", CP(embf[0:16, :], emf[:]), reads=["emf", "embf"], writes=["embf"])
            P.op("pool", lambda e: e.memset(ones[:], 1.0), writes=["ones"])
            P.op("pool", lambda e: e.memset(mhalf[:], -0.5), writes=["mhalf"])
            P.flush()

        def phase_A(l, xsrc, hT):
            with ExitStack() as es:
                xin = [sb(es, nc, f"A_x{i}", [128, D], F32) for i in range(3)]
                hb = [sb(es, nc, f"A_hb{i}", [128, D], BF16) for i in range(2)]
                jk = sb(es, nc, "A_jk", [128, D], BF16)
                gbc = sb(es, nc, "A_g", [128, D], F32)
                ssum = sb(es, nc, "A_ss", [128, NT128], F32)
                sq = sb(es, nc, "A_sq", [128, NT128], F32)
                rsd = sb(es, nc, "A_rs", [128, NT128], F32)
                pt = [ps(es, nc, f"A_pt{i}", [128, 1024], BF16) for i in range(4)]
                P.dma("sync", gbc[:], norm_g[l].partition_broadcast(128), writes=["Ag"], key="Ag")
                def loadA(i):
                    P.dma("sync", xin[i % 3][:], xsrc[i * 128:(i + 1) * 128, :],
                          writes=["Ax%d" % (i % 3)], key="Ax%d" % (i % 3))

                def normA1(i):
                    xt_ = xin[i % 3]
                    P.op("dve", STT(jk[:], xt_[:], 1.0, xt_[:], ALU.mult, ALU.mult, accum_out=ssum[:, i:i + 1]),
                         reads=["Ax%d" % (i % 3)], writes=["Ajk", "Ass%d" % i])
                    P.op("dve", TS(ssum[:, i:i + 1], ssum[:, i:i + 1], 1.0 / D, EPS, ALU.mult, ALU.add),
                         reads=["Ass%d" % i], writes=["Ass%d" % i])
                    P.op("pool", TT(rsd[:, i:i + 1], ssum[:, i:i + 1], mhalf[:, 0:1], ALU.pow),
                         reads=["Ass%d" % i, "mhalf"], writes=["Ars%d" % i])

                def normA2(i):
                    xt_ = xin[i % 3]
                    P.op("dve", STT(hb[i % 2][:], xt_[:], rsd[:, i:i + 1], gbc[:], ALU.mult, ALU.mult),
                         reads=["Ax%d" % (i % 3), "Ars%d" % i, "Ag"], writes=["Ahb%d" % (i % 2)])

                loadA(0)
                if NT128 > 1:
                    loadA(1)
                normA1(0)
                for i in range(NT128):
                    if i + 2 < NT128:
                        loadA(i + 2)
                    if i + 1 < NT128:
                        normA1(i + 1)
                    normA2(i)
                    for half in range(2):
                        pti = (2 * i + half) % 4
                        for j in range(8):
                            kc = half * 8 + j
                            P.op("pe", TR(pt[pti][:, j * 128:(j + 1) * 128], hb[i % 2][:, kc * 128:(kc + 1) * 128], ident[:]),
                                 reads=["Ahb%d" % (i % 2), "ident"], writes=["Apt%d" % pti], signal=(j == 7))
                        P.op("act", ACT(hT[:, half * 8:(half + 1) * 8, i * 128:(i + 1) * 128],
                                        pt[pti][:].rearrange("p (j t) -> p j t", t=128), AF.Copy),
                             reads=["Apt%d" % pti], writes=["hT"])
                P.flush()

        def norm_tile_eps(P_, tag, xt, gbc, hb, ssum, sq, rsd, jk, i, xkey=None):
            xkey = xkey or (tag + "x%d" % (i % 2))
            P_.op("dve", STT(jk[:], xt[:], 1.0, xt[:], ALU.mult, ALU.mult, accum_out=ssum[:, i:i + 1]),
                  reads=[xkey], writes=[tag + "jk", tag + "ss%d" % i])
            P_.op("dve", TS(ssum[:, i:i + 1], ssum[:, i:i + 1], 1.0 / D, EPS, ALU.mult, ALU.add),
                  reads=[tag + "ss%d" % i], writes=[tag + "ss%d" % i])
            P_.op("pool", TT(rsd[:, i:i + 1], ssum[:, i:i + 1], mhalf[:, 0:1], ALU.pow),
                  reads=[tag + "ss%d" % i, "mhalf"], writes=[tag + "rs%d" % i])
            P_.op("dve", STT(hb[:], xt[:], rsd[:, i:i + 1], gbc[:], ALU.mult, ALU.mult),
                  reads=[xkey, tag + "rs%d" % i, tag + "g"],
                  writes=[tag + "hb%d" % (i % 2)])

        def phase_B(l, hT):
            jobs = []
            def fm(name, dest, func, scale=None, dt=BF16, width=None):
                c0 = OFF[name]
                width_ = width or 1024
                for s in range(0, width_, 512):
                    n = min(512, width_ - s)
                    jobs.append(dict(c0=c0 + s, n=n, kind="fm", dest=dest, r0=s, func=func, scale=scale, dt=dt))
            def tm(name, dest, width):
                c0 = OFF[name]
                for s in range(0, width, 512):
                    n = min(512, width - s)
                    jobs.append(dict(c0=c0 + s, n=n, kind="tm", dest=dest, r0=s))
            fm("qa", qaT, AF.Copy, scale=128 ** -0.5)
            fm("ka", kaT, AF.Copy)
            tm("va", va, 1024)
            fm("ga", gaT, AF.Silu)
            fm("qb", qbT, AF.Copy, scale=0.125)
            fm("kb", kbT, AF.Copy, width=256)
            tm("vb", vb, 256)
            fm("gb", gbT, AF.Silu)
            fm("xc", xcT, AF.Copy, dt=F32)
            fm("gc", gcT, AF.Silu)
            fm("mg", mgT, AF.Sigmoid, width=3 * D)
            TH = min(T, 2048)
            with ExitStack() as es:
                wt = [sb(es, nc, f"B_w{i}", [128, 16, 512], BF16) for i in range(2)]
                obb = [sb(es, nc, f"B_ob{i}", [128, TH], BF16) for i in range(2)]
                obf = [sb(es, nc, f"B_of{i}", [128, TH], F32) for i in range(2)]
                obt = [sb(es, nc, f"B_ot{i}", [128, 4, 512], BF16) for i in range(2)]
                pb = [ps(es, nc, f"B_p{i}", [128, 512], F32) for i in range(4)]
                pk = 0
                kb_ = 0
                kf_ = 0
                kt_ = 0
                wview = w_in[l].rearrange("(kc p) n -> p kc n", p=128)
                for ji, jb in enumerate(jobs):
                    w = wt[ji % 2]
                    wk = "Bw%d" % (ji % 2)
                    n = jb["n"]
                    P.dma("pool", w[:, :, 0:n], wview[:, :, jb["c0"]:jb["c0"] + n], writes=[wk], key=wk)
                    if jb["kind"] == "fm":
                        for cc in range(n // 128):
                            for half in range(T // TH):
                                if jb["dt"] == F32:
                                    ob = obf[kf_ % 2]; obk = "Bof%d" % (kf_ % 2); kf_ += 1
                                else:
                                    ob = obb[kb_ % 2]; obk = "Bob%d" % (kb_ % 2); kb_ += 1
                                for tt in range(TH // 512):
                                    t0 = half * TH + tt * 512
                                    pbk = "Bp%d" % (pk % 4); pbt = pb[pk % 4]; pk += 1
                                    for kc in range(16):
                                        P.op("pe", MM(pbt[:, :], w[:, kc, cc * 128:(cc + 1) * 128], hT[:, kc, t0:t0 + 512],
                                                      kc == 0, kc == 15),
                                             reads=[wk, "hT"], writes=[pbk], signal=(kc == 15))
                                    P.op("act", ACT(ob[:, tt * 512:(tt + 1) * 512], pbt[:, :], jb["func"], scale=jb["scale"]),
                                         reads=[pbk], writes=[obk])
                                r = jb["r0"] + cc * 128
                                P.dma("sync", jb["dest"][r:r + 128, half * TH:(half + 1) * TH], ob[:, :],
                                      reads=[obk], key=obk)
                    else:
                        for tg in range(T // 512):
                            ot = obt[kt_ % 2]; otk = "Bot%d" % (kt_ % 2); kt_ += 1
                            for ti in range(4):
                                tok0 = tg * 512 + ti * 128
                                pbk = "Bp%d" % (pk % 4); pbt = pb[pk % 4]; pk += 1
                                for kc in range(16):
                                    P.op("pe", MM(pbt[:, 0:n], hT[:, kc, tok0:tok0 + 128], w[:, kc, 0:n], kc == 0, kc == 15),
                                         reads=[wk, "hT"], writes=[pbk], signal=(kc == 15))
                                P.op("act", ACT(ot[:, ti, 0:n], pbt[:, 0:n], AF.Copy), reads=[pbk], writes=[otk])
                            c = jb["r0"]
                            P.dma("sync", jb["dest"][tg * 512:(tg + 1) * 512, c:c + n].rearrange("(a q) c -> q a c", q=128),
                                  ot[:, :, 0:n], reads=[otk], key=otk)
                P.flush()

        def phase_C(l):
            with ExitStack() as es:
                cv = sb(es, nc, "C_cv", [128, 8, 8], F32)
                e1 = sb(es, nc, "C_e1", [128, 8], F32)
                n8 = sb(es, nc, "C_n8", [128, 8], F32)
                n16 = sb(es, nc, "C_n16", [128, 8], F32)
                wr = sb(es, nc, "C_wr", [128, 8, 128], BF16)
                wi = sb(es, nc, "C_wi", [128, 8, 128], BF16)
                xt1 = sb(es, nc, "C_x", [128, T], F32)
                gct = [sb(es, nc, f"C_g{i}", [128, T], BF16) for i in range(2)]
                ot1 = sb(es, nc, "C_o", [128, T], BF16)
                u2 = [sb(es, nc, f"C_u{i}", [128, T], F32) for i in range(2)]
                ub1 = sb(es, nc, "C_ub", [128, T], BF16)
                rt2 = [sb(es, nc, f"C_r{i}", [128, T], F32) for i in range(2)]
                it2 = [sb(es, nc, f"C_i{i}", [128, T], F32) for i in range(2)]
                at2 = [sb(es, nc, f"C_a{i}", [128, T], F32) for i in range(2)]
                ht = sb(es, nc, "C_h", [128, T], F32)
                pr = [ps(es, nc, f"C_pr{i}", [128, 512], F32) for i in range(2)]
                pi = [ps(es, nc, f"C_pi{i}", [128, 512], F32) for i in range(2)]
                P.dma("sync", cv[:], cvec[l], writes=["cv"], key="Ccv")
                P.dma("pool", wr[:], w_r[l].rearrange("n c d -> c n d"), writes=["wr"], key="Cwr")
                P.dma("pool", wi[:], w_i[l].rearrange("n c d -> c n d"), writes=["wi"], key="Cwi")
                P.op("act", ACT(e1[:], cv[:, :, 7], AF.Exp, scale=-1.0), reads=["cv"], writes=["e1"])
                P.op("act", ACT(e1[:], e1[:], AF.Ln, bias=1.0), reads=["e1"], writes=["e1"])
                P.op("dve", TS(n8[:], e1[:], -8.0, None, ALU.mult), reads=["e1"], writes=["n8"])
                P.op("dve", TS(n16[:], e1[:], -16.0, None, ALU.mult), reads=["e1"], writes=["n16"])

                def loadX(n):
                    P.dma("sync", xt1[:], xcT[n * 128:(n + 1) * 128, :], writes=["Cx"], key="Cx")

                def loadG(n):
                    b = n % 2
                    P.dma("sync", gct[b][:], gcT[n * 128:(n + 1) * 128, :], writes=["Cg%d" % b], key="Cg%d" % b)

                def stage1(n):
                    b = n % 2
                    X = xt1
                    u = u2[b]
                    uk = "u%d" % b
                    P.op("dve", TS(u[:], X[:], cv[:, n, 3:4], cv[:, n, 4:5], ALU.mult, ALU.add),
                         reads=["Cx", "cv"], writes=[uk])
                    for s_, wtap in ((1, 2), (2, 1), (3, 0)):
                        P.op("dve", STT(u[:, s_:T], X[:, 0:T - s_], cv[:, n, wtap:wtap + 1], u[:, s_:T], ALU.mult, ALU.add),
                             reads=["Cx", "cv", uk], writes=[uk])

                def stage2a(n):
                    b = n % 2
                    rt, it, at = rt2[b], it2[b], at2[b]
                    rk, ik, ak = "r%d" % b, "i%d" % b, "a%d" % b
                    P.op("act", ACT(ub1[:], u2[b][:], AF.Copy), reads=["u%d" % b], writes=["ub"])
                    for tt in range(NT512):
                        sl = slice(tt * 512, (tt + 1) * 512)
                        k = tt % 2
                        P.op("pe", MM(pr[k][:, :], wr[:, n, :], ub1[:, sl], True, True), reads=["wr", "ub"], writes=["Cpr%d" % k])
                        P.op("pe", MM(pi[k][:, :], wi[:, n, :], ub1[:, sl], True, True), reads=["wi", "ub"], writes=["Cpi%d" % k])
                        P.op("act", ACT(rt[:, sl], pr[k][:, :], AF.Sigmoid, bias=cv[:, n, 5:6]),
                             reads=["Cpr%d" % k, "cv"], writes=[rk])
                        P.op("act", ACT(it[:, sl], pi[k][:, :], AF.Sigmoid, bias=cv[:, n, 6:7]),
                             reads=["Cpi%d" % k, "cv"], writes=[ik])
                    P.op("pool", TT(it[:], it[:], u2[b][:], ALU.mult), reads=[ik, "u%d" % b], writes=[ik])
                    P.op("act", ACT(at[:], rt[:], AF.Exp, scale=n8[:, n:n + 1]), reads=[rk, "n8"], writes=[ak])
                    P.op("act", ACT(rt[:], rt[:], AF.Exp, scale=n16[:, n:n + 1]), reads=[rk, "n16"], writes=[rk])
                    P.op("act", ACT(rt[:], rt[:], AF.Sqrt, bias=1.0, scale=-1.0), reads=[rk], writes=[rk])

                def stage2b(n):
                    b = n % 2
                    rt, it, at = rt2[b], it2[b], at2[b]
                    rk, ik, ak = "r%d" % b, "i%d" % b, "a%d" % b
                    P.op("dve", TT(it[:], it[:], rt[:], ALU.mult), reads=[ik, rk], writes=[ik])
                    P.op("dve", lambda e, a_=at, i_=it, h_=ht: e.tensor_tensor_scan(out=h_[:], data0=a_[:], data1=i_[:], initial=0.0,
                                                                                   op0=ALU.mult, op1=ALU.add),
                         reads=[ak, ik], writes=["h"])
                    P.op("pool", TT(ot1[:], ht[:], gct[b][:], ALU.mult), reads=["h", "Cg%d" % b], writes=["Co"])
                    P.dma("sync", ygT[2048 + n * 128:2048 + (n + 1) * 128, :], ot1[:], reads=["Co"], key="Co")

                loadX(0)
                loadG(0)
                stage1(0)
                loadX(1)
                stage2a(0)
                stage1(1)
                for n in range(8):
                    if n + 2 < 8:
                        loadX(n + 2)
                    if n + 1 < 8:
                        loadG(n + 1)
                        stage2a(n + 1)
                    if n + 2 < 8:
                        stage1(n + 2)
                    stage2b(n)
                P.flush()

        def phase_D(l):
            with ExitStack() as es:
                bBf = sb(es, nc, "D_bf", [128, 2, 16, 128], F32)
                mBf = sb(es, nc, "D_mf", [128, 2, 16, 128], F32)
                bB = sb(es, nc, "D_bb", [128, 2, 16, 128], BF16)
                sk = sb(es, nc, "D_sk", [128, 16], F32)
                esk = sb(es, nc, "D_es", [128, 16], F32)
                idf32 = sb(es, nc, "D_idf", [128, 128], F32)
                qg2 = [sb(es, nc, f"D_q{i}", [64, 4, T], BF16) for i in range(2)]
                gg2 = sb(es, nc, "D_g", [128, 2, T], BF16)
                og2 = sb(es, nc, "D_o", [128, 2, T], BF16)
                kg2 = [sb(es, nc, f"D_k{i}", [64, T], BF16) for i in range(2)]
                vaug2 = [sb(es, nc, f"D_v{i}", [128, NT128, 65], BF16) for i in range(2)]
                pto = [sb(es, nc, f"D_pto{i}", [128, 4, 128], BF16) for i in range(2)]
                ptp = [sb(es, nc, f"D_ptp{i}", [128, 4, 128], BF16) for i in range(2)]
                dn = [sb(es, nc, f"D_dn{i}", [128, 4], F32) for i in range(2)]
                rdn = [sb(es, nc, f"D_rdn{i}", [128, 4], F32) for i in range(2)]
                yn = [sb(es, nc, f"D_yn{i}", [128, 256], F32) for i in range(2)]
                sto = [ps(es, nc, f"D_so{i}", [128, 512], F32) for i in range(2)]
                stp = [ps(es, nc, f"D_sp{i}", [128, 512], F32) for i in range(2)]
                accp = [ps(es, nc, f"D_ac{i}", [128, 512], F32) for i in range(2)]
                ytr = [ps(es, nc, f"D_yt{i}", [128, 512], F32) for i in range(2)]
                P.dma("sync", bBf[:], biasB[:, :, :, :], writes=["bBf"], key="Dbf")
                P.dma("sync", mBf[:], maskB[:, :, :, :], writes=["mBf"], key="Dmf")
                P.dma("sync", sk[:], sinks[l].partition_broadcast(128), writes=["sk"], key="Dsk")
                P.dma("sync", idf32[:], identf[:, :], writes=["idf32"], key="Did")
                P.op("dve", TT(bB[:], bBf[:], mBf[:], ALU.add), reads=["bBf", "mBf"], writes=["bB"])
                P.op("act", ACT(esk[:], sk[:], AF.Exp), reads=["sk"], writes=["esk"])
                for i_ in range(2):
                    P.op("pool", lambda e, i_=i_: e.memset(vaug2[i_][:, :, 64:65], 1.0), writes=["vg%d" % i_])

                def emit_st(g, j):
                    b = j % 2
                    qg, kg = qg2[g % 2], kg2[g % 2]
                    qs = slice(j * 128, (j + 1) * 128)
                    so = sto[b][:].rearrange("p (h q) -> p h q", q=128)
                    P.op("pe", MM(so, kg[:, qs], qg[:, :, qs], True, False), reads=["kg%d" % (g % 2), "qg%d" % (g % 2)], writes=["Dso%d" % b], signal=False)
                    P.op("pe", MM(so, ident[:], bB[:, 1, 4 * g:4 * g + 4, :], False, True), reads=["ident", "bB"], writes=["Dso%d" % b])
                    P.op("act", ACT(pto[b][:], so, AF.Exp), reads=["Dso%d" % b], writes=["Dpto%d" % b])
                    if j > 0:
                        sp = stp[b][:].rearrange("p (h q) -> p h q", q=128)
                        ks = slice((j - 1) * 128, j * 128)
                        P.op("pe", MM(sp, kg[:, ks], qg[:, :, qs], True, False), reads=["kg%d" % (g % 2), "qg%d" % (g % 2)], writes=["Dsp%d" % b], signal=False)
                        P.op("pe", MM(sp, ident[:], bB[:, 0, 4 * g:4 * g + 4, :], False, True), reads=["ident", "bB"], writes=["Dsp%d" % b])
                        P.op("act", ACT(ptp[b][:], sp, AF.Exp), reads=["Dsp%d" % b], writes=["Dptp%d" % b])

                def emit_pv(g, j):
                    b = j % 2
                    qs = slice(j * 128, (j + 1) * 128)
                    av = accp[b][:, 0:260].rearrange("p (h c) -> p h c", c=65)
                    ak = "Dac%d" % b
                    vaug = vaug2[g % 2]
                    vgk = "vg%d" % (g % 2)
                    for h in range(4):
                        if j > 0:
                            P.op("pe", (lambda e, av=av, h=h, b=b, j=j, vaug=vaug: e.matmul(av[:, h, :], ptp[b][:, h, :], vaug[:, j - 1, :],
                                                                             start=(h == 0), stop=False, skip_group_check=True)),
                                 reads=[vgk, "Dptp%d" % b], writes=[ak], signal=False)
                            P.op("pe", (lambda e, av=av, h=h, b=b, j=j, vaug=vaug: e.matmul(av[:, h, :], pto[b][:, h, :], vaug[:, j, :],
                                                                             start=False, stop=True, skip_group_check=True)),
                                 reads=[vgk, "Dpto%d" % b], writes=[ak], signal=(h == 3))
                        else:
                            P.op("pe", (lambda e, av=av, h=h, b=b, j=j, vaug=vaug: e.matmul(av[:, h, :], pto[b][:, h, :], vaug[:, j, :],
                                                                             start=(h == 0), stop=True, skip_group_check=True)),
                                 reads=[vgk, "Dpto%d" % b], writes=[ak], signal=(h == 3))
                    P.op("dve", TT(dn[b][:], av[:, :, 64], esk[:, 4 * g:4 * g + 4], ALU.add), reads=[ak, "esk"], writes=["Ddn%d" % b])
                    P.op("dve", lambda e, b=b: e.reciprocal(out=rdn[b][:], in_=dn[b][:]), reads=["Ddn%d" % b], writes=["Drdn%d" % b])
                    P.op("dve", TT(yn[b][:].rearrange("p (h c) -> p h c", c=64), av[:, :, 0:64],
                                   rdn[b][:].unsqueeze(2).broadcast_to([128, 4, 64]), ALU.mult),
                         reads=[ak, "Drdn%d" % b], writes=["Dyn%d" % b])

                def emit_tr(g, j):
                    b = j % 2
                    qs = slice(j * 128, (j + 1) * 128)
                    for pr_ in range(2):
                        P.op("pe", TR(ytr[b][:, pr_ * 128:(pr_ + 1) * 128], yn[b][:, pr_ * 128:(pr_ + 1) * 128], idf32[:]),
                             reads=["Dyn%d" % b, "idf32"], writes=["Dyt%d" % b], signal=(pr_ == 1))
                    P.op("dve", TT(og2[:, :, qs], ytr[b][:, 0:256].rearrange("p (r q) -> p r q", q=128), gg2[:, :, qs], ALU.mult),
                         reads=["Dyt%d" % b, "gg"], writes=["og"])

                def loadD(g):
                    i_ = g % 2
                    P.dma("sync", kg2[i_][:], kbT[g * 64:(g + 1) * 64, :], writes=["kg%d" % i_], key="Dk%d" % i_)
                    P.dma("sync", qg2[i_][:], qbT[g * 256:(g + 1) * 256, :].rearrange("(h d) t -> d h t", d=64),
                          writes=["qg%d" % i_], key="Dq%d" % i_)
                    P.dma("sync", vaug2[i_][:, :, 0:64], vb[:, g * 64:(g + 1) * 64].rearrange("(c k) d -> k c d", k=128),
                          writes=["vg%d" % i_], key="Dv%d" % i_)

                loadD(0)
                for g in range(4):
                    P.dma("sync", gg2[:], gbT[g * 256:(g + 1) * 256, :].rearrange("(r q) t -> q r t", q=128), writes=["gg"], key="Dg")
                    if g + 1 < 4:
                        loadD(g + 1)
                    emit_st(g, 0)
                    for j in range(NT128):
                        if j + 1 < NT128:
                            emit_st(g, j + 1)
                        emit_pv(g, j)
                        if j > 0:
                            emit_tr(g, j - 1)
                    emit_tr(g, NT128 - 1)
                    P.dma("sync", ygT[1024 + g * 256:1024 + (g + 1) * 256, :].rearrange("(r q) t -> q r t", q=128), og2[:],
                          reads=["og"], key="Do")
                P.flush()

        def phase_E(l):
            with ExitStack() as es:
                mAf = sb(es, nc, "E_mf", [128, 6, 512], F32)
                bAf = sb(es, nc, "E_bf", [128, 6, 512], F32)
                bA = sb(es, nc, "E_bb", [128, 6, 512], BF16)
                c31s = sb(es, nc, "E_c31", [128, 8], F32)
                pms = sb(es, nc, "E_pm", [128, NT128 * 16], F32)
                idf32 = sb(es, nc, "E_idf", [128, 128], F32)
                qh = [sb(es, nc, f"E_q{i}", [128, T], BF16) for i in range(2)]
                kh = [sb(es, nc, f"E_k{i}", [128, T], BF16) for i in range(2)]
                vh = [sb(es, nc, f"E_v{i}", [128, NT128, 129], BF16) for i in range(2)]
                gh = [sb(es, nc, f"E_g{i}", [128, T], BF16) for i in range(2)]
                oh = [sb(es, nc, f"E_o{i}", [128, T], BF16) for i in range(2)]
                ksum = sb(es, nc, "E_ks", [128, 16], F32)
                kmb = sb(es, nc, "E_km", [128, 16], BF16)
                gm = sb(es, nc, "E_gm", [128, NT128 * 16], F32)
                top8 = sb(es, nc, "E_t8", [128, NT128, 8], F32)
                msel = sb(es, nc, "E_ms", [128, NT128, 16], F32)
                mselT = sb(es, nc, "E_mt", [128, T], BF16)
                pt = [sb(es, nc, f"E_pt{i}", [128, 512], BF16) for i in range(4)]
                rdn = sb(es, nc, "E_rd", [128, 8], F32)
                yn = [sb(es, nc, f"E_yn{i}", [128, 128], F32) for i in range(4)]
                trp = ps(es, nc, "E_trp", [128, 512], F32)
                stp = [ps(es, nc, f"E_st{i}", [128, 512], F32) for i in range(3)]
                gps = stp[2]
                acc = [[ps(es, nc, f"E_acc{i}_{k}", [128, 512], F32) for k in range(2)] for i in range(2)]
                P.dma("sync", mAf[:], maskA[:, :, :], writes=["mAf"], key="Emf")
                P.dma("sync", c31s[:], c31[:, :], writes=["c31"], key="Ec31")
                P.dma("sync", pms[:], pastm[:, :], writes=["pms"], key="Epm")
                P.dma("sync", idf32[:], identf[:, :], writes=["idf32"], key="Eid")
                P.op("pool", lambda e: e.memset(ksum[:], 0.0), writes=["ksum"])
                P.op("pool", lambda e: e.memset(mselT[:], 0.0), writes=["mselT"])
                for i_ in range(2):
                    P.op("pool", lambda e, i_=i_: e.memset(vh[i_][:, :, 128:129], 1.0), writes=["Ev%d" % i_])

                def loadE(h):
                    b = h % 2
                    P.dma("sync", qh[b][:], qaT[h * 128:(h + 1) * 128, :], writes=["Eq%d" % b], key="Eq%d" % b)
                    P.dma("sync", kh[b][:], kaT[h * 128:(h + 1) * 128, :], writes=["Ek%d" % b], key="Ek%d" % b)
                    P.dma("sync", vh[b][:, :, 0:128], va[:, h * 128:(h + 1) * 128].rearrange("(c k) d -> k c d", k=128),
                          writes=["Ev%d" % b], key="Ev%d" % b)
                    P.dma("sync", gh[b][:], gaT[h * 128:(h + 1) * 128, :], writes=["Eg%d" % b], key="Eg%d" % b)

                sti = [0]
                pti = [0]
                oi = [0]
                yi = [0]
                loadE(0)
                P.dma("sync", bAf[:], biasA[0], writes=["bAf"], key="Ebf")
                for h in range(8):
                    b = h % 2
                    Q, K, V, G, O = qh[b], kh[b], vh[b], gh[b], oh[b]
                    qk, kk, vk, gk, ok = "Eq%d" % b, "Ek%d" % b, "Ev%d" % b, "Eg%d" % b, "Eo%d" % b
                    P.op("dve", TT(bA[:], bAf[:], mAf[:], ALU.add), reads=["bAf", "mAf"], writes=["bA"])
                    if h + 1 < 8:
                        loadE(h + 1)
                        P.dma("sync", bAf[:], biasA[h + 1], writes=["bAf"], key="Ebf")
                    P.op("dve", lambda e, K=K: e.tensor_reduce(out=ksum[:, 0:NBLK], in_=K[:].rearrange("p (n l) -> p n l", l=256),
                                                              axis=AX.X, op=ALU.add),
                         reads=[kk], writes=["ksum"])
                    P.op("act", ACT(kmb[:], ksum[:], AF.Copy, scale=1.0 / 256), reads=["ksum"], writes=["kmb"])
                    for i in range(NT128):
                        P.op("pe", MM(gps[:, i * 16:(i + 1) * 16], Q[:, i * 128:(i + 1) * 128], kmb[:], True, True),
                             reads=[qk, "kmb"], writes=["Est2"], signal=(i == NT128 - 1))
                    P.op("dve", TT(gm[:], gps[:, 0:NT128 * 16], pms[:], ALU.add), reads=["Est2", "pms"], writes=["gm"])
                    for i in range(NT128):
                        P.op("dve", lambda e, i=i: e.max(out=top8[:, i, :], in_=gm[:, i * 16:(i + 1) * 16]),
                             reads=["gm"], writes=["t8_%d" % i])
                        P.op("dve", TS(msel[:, i, :], gm[:, i * 16:(i + 1) * 16], top8[:, i, 3:4], 1.0, ALU.is_ge, ALU.subtract),
                             reads=["gm", "t8_%d" % i], writes=["ms_%d" % i])
                    for qt in range(NT512):
                        for k4 in range(4):
                            i = qt * 4 + k4
                            P.op("pe", TR(trp[0:16, k4 * 128:(k4 + 1) * 128], msel[:, i, :], idf32[:]),
                                 reads=["ms_%d" % i, "idf32"], writes=["trp"], signal=(k4 == 3))
                        P.op("act", ACT(mselT[0:16, qt * 512:(qt + 1) * 512], trp[0:16, 0:512], AF.Copy),
                             reads=["trp"], writes=["mselT"])
                    tiles = [(qt, kc) for qt in range(NT512) for kc in range(4 * (qt + 1))]

                    def emit_st(qt, kc):
                        qs = slice(qt * 512, (qt + 1) * 512)
                        n = kc // 2
                        jrel = kc - (4 * qt - 2)
                        near = 0 <= jrel < 6
                        s_ = sti[0] % 3
                        sti[0] += 1
                        stk = "Est%d" % s_
                        P.op("pe", MM(stp[s_][:, :], K[:, kc * 128:(kc + 1) * 128], Q[:, qs], True, False),
                             reads=[kk, qk], writes=[stk], signal=False)
                        P.op("pe", MM(stp[s_][:, :], embf[:, n * 128:(n + 1) * 128], mselT[:, qs], False, not near),
                             reads=["embf", "mselT"], writes=[stk], signal=(not near))
                        if near:
                            P.op("pe", MM(stp[s_][:, :], ident[:], bA[:, jrel, :], False, True),
                                 reads=["ident", "bA"], writes=[stk])
                        p_ = pti[0] % 4
                        pti[0] += 1
                        ptk = "Ept%d" % p_
                        if near:
                            P.op("act", ACT(pt[p_][:], stp[s_][:, :], AF.Exp), reads=[stk], writes=[ptk])
                        else:
                            P.op("act", ACT(pt[p_][:], stp[s_][:, :], AF.Exp, bias=c31s[:, h:h + 1]),
                                 reads=[stk, "c31"], writes=[ptk])
                        return p_

                    def emit_pv(qt, kc, p_, ob_):
                        nkc = 4 * (qt + 1)
                        qs = slice(qt * 512, (qt + 1) * 512)
                        ptk = "Ept%d" % p_
                        last = kc == nkc - 1
                        for sub in range(4):
                            bank = acc[ob_][sub // 2]
                            c0 = (sub % 2) * 129
                            bkey = "Eacc%d_%d" % (ob_, sub // 2)
                            P.op("pe", (lambda e, bank=bank, c0=c0, sub=sub, p_=p_, kc=kc, last=last, V=V:
                                        e.matmul(bank[:, c0:c0 + 129], pt[p_][:, sub * 128:(sub + 1) * 128], V[:, kc, :],
                                                 start=(kc == 0 and sub % 2 == 0), stop=last, skip_group_check=True)),
                                 reads=[vk, ptk], writes=[bkey], signal=(sub == 3))
                        if last:
                            for sub in range(4):
                                bank = acc[ob_][sub // 2]
                                c0 = (sub % 2) * 129
                                bkey = "Eacc%d_%d" % (ob_, sub // 2)
                                y_ = sub
                                P.op("dve", lambda e, bank=bank, c0=c0, sub=sub: e.reciprocal(out=rdn[:, sub:sub + 1], in_=bank[:, c0 + 128:c0 + 129]),
                                     reads=[bkey], writes=["rdn%d" % sub])
                                P.op("dve", TS(yn[y_][:], bank[:, c0:c0 + 128], rdn[:, sub:sub + 1], None, ALU.mult),
                                     reads=[bkey, "rdn%d" % sub], writes=["Eyn%d" % y_])

                    def emit_ep(qt):
                        qs = slice(qt * 512, (qt + 1) * 512)
                        for sub in range(4):
                            P.op("pe", TR(trp[:, sub * 128:(sub + 1) * 128], yn[sub][:], idf32[:]),
                                 reads=["Eyn%d" % sub, "idf32"], writes=["trp"], signal=(sub == 3))
                        P.op("dve", TT(O[:, qs], trp[:, :], G[:, qs], ALU.mult), reads=["trp", gk], writes=[ok])

                    pq_ = [emit_st(*tiles[0])]
                    if len(tiles) > 1:
                        pq_.append(emit_st(*tiles[1]))
                    pend = None
                    for ti, (qt, kc) in enumerate(tiles):
                        if kc == 0:
                            oi[0] += 1
                        if ti + 2 < len(tiles):
                            pq_.append(emit_st(*tiles[ti + 2]))
                        pcur = pq_.pop(0)
                        emit_pv(qt, kc, pcur, oi[0] % 2)
                        if pend is not None:
                            emit_ep(pend)
                            pend = None
                        if kc == 4 * (qt + 1) - 1:
                            pend = qt
                    if pend is not None:
                        emit_ep(pend)
                    P.dma("sync", ygT[h * 128:(h + 1) * 128, :], O[:], reads=[ok], key=ok)
                P.flush()

        def phase_F1(l):
            with ExitStack() as es:
                wb = sb(es, nc, "F_wb", [128, 24, D], BF16)
                yg = [sb(es, nc, f"F_yg{i}", [128, 24, 512], BF16) for i in range(2)]
                gt = [sb(es, nc, f"F_gt{i}", [128, 3, 512], BF16) for i in range(2)]
                m = [sb(es, nc, f"F_m{i}", [128, 512], F32) for i in range(3)]
                mo = sb(es, nc, "F_mo", [128, 16, 512], BF16)
                pp = [[ps(es, nc, f"F_p{i}_{n}", [128, 512], F32) for n in range(3)] for i in range(2)]
                for n in range(3):
                    for hh in range(2):
                        P.dma("pool", wb[:, n * 8 + hh * 4:n * 8 + hh * 4 + 4, :],
                              w_br[l, n, hh * 512:(hh + 1) * 512, :].rearrange("(cc q) d -> q cc d", q=128),
                              writes=["wb%d" % (n * 2 + hh)], key="Fwb%d" % (n * 2 + hh))
                it = 0
                def loadYG(tt):
                    yb = tt % 2
                    P.dma("sync", yg[yb][:], ygT[:, tt * 512:(tt + 1) * 512].rearrange("(c q) t -> q c t", q=128),
                          writes=["Fyg%d" % yb], key="Fyg%d" % yb)
                def loadGT(it_):
                    tt_, j_ = divmod(it_, 16)
                    b_ = it_ % 2
                    P.dma("sync", gt[b_][:],
                          mgT[:, tt_ * 512:(tt_ + 1) * 512].rearrange("(n r) t -> r n t", n=3)[j_ * 128:(j_ + 1) * 128],
                          writes=["Fgt%d" % b_], key="Fgt%d" % b_)
                loadYG(0)
                loadGT(0)
                for tt in range(NT512):
                    ts = slice(tt * 512, (tt + 1) * 512)
                    yb = tt % 2
                    if tt + 1 < NT512:
                        loadYG(tt + 1)
                    for j in range(16):
                        b = it % 2
                        it += 1
                        if it < NT512 * 16:
                            loadGT(it)
                        for n in range(3):
                            for cc in range(8):
                                P.op("pe", MM(pp[b][n][:, :], wb[:, n * 8 + cc, j * 128:(j + 1) * 128], yg[yb][:, n * 8 + cc, :],
                                              cc == 0, cc == 7),
                                     reads=["wb%d" % (n * 2 + cc // 4), "Fyg%d" % yb], writes=["Fp%d_%d" % (b, n)], signal=(cc == 7))
                            P.op("dve", TT(m[n][:], pp[b][n][:, :], gt[b][:, n, :], ALU.mult),
                                 reads=["Fp%d_%d" % (b, n), "Fgt%d" % b], writes=["Fm%d" % n])
                        P.op("pool", TT(m[0][:], m[0][:], m[1][:], ALU.add), reads=["Fm0", "Fm1"], writes=["Fm0"])
                        P.op("pool", TT(mo[:, j, :], m[0][:], m[2][:], ALU.add), reads=["Fm0", "Fm2"], writes=["Fmo"])
                    P.dma("sync", mrgT[:, ts].rearrange("(j q) t -> q j t", q=128), mo[:], reads=["Fmo"], key="Fmo")
                P.flush()

        def load_G_weights(l, wg, wp):
            for kq in range(4):
                P.dma("pool", wg[:, kq * 4:(kq + 1) * 4, :],
                      w_pg[l, kq * 512:(kq + 1) * 512, :].rearrange("(kc q) d -> q kc d", q=128),
                      writes=["wg"], key="Hwg%d" % kq)
            P.dma("pool", wp[:], w_pp[l].rearrange("(kc q) d -> q kc d", q=128), writes=["wp"], key="Hwp")

        def phase_F2(l, xsrc, wg, wp):
            with ExitStack() as es:
                wo = sb(es, nc, "G_wo", [128, 16, D], BF16)
                mt = [sb(es, nc, f"G_mt{i}", [128, 16, 512], BF16) for i in range(2)]
                xi = [sb(es, nc, f"G_xi{i}", [128, D], F32) for i in range(2)]
                xo = [sb(es, nc, f"G_xo{i}", [128, D], F32) for i in range(2)]
                pq = [ps(es, nc, f"G_p{i}", [128, 512], F32) for i in range(4)]
                for kq in range(4):
                    P.dma("pool", wo[:, kq * 4:(kq + 1) * 4, :],
                          w_out[l, kq * 512:(kq + 1) * 512, :].rearrange("(kc q) d -> q kc d", q=128),
                          writes=["wo%d" % kq], key="Gwo%d" % kq)
                load_G_weights(l, wg, wp)
                pk = 0
                def loadMT(tt_):
                    mb_ = tt_ % 2
                    P.dma("sync", mt[mb_][:], mrgT[:, tt_ * 512:(tt_ + 1) * 512].rearrange("(kc q) t -> q kc t", q=128),
                          writes=["Gmt%d" % mb_], key="Gmt%d" % mb_)
                def loadXI(i_):
                    b_ = i_ % 2
                    P.dma("sync", xi[b_][:], xsrc[i_ * 128:(i_ + 1) * 128, :], writes=["Gxi%d" % b_], key="Gxi%d" % b_)
                loadMT(0)
                loadXI(0)
                for tt in range(NT512):
                    mb = tt % 2
                    if tt + 1 < NT512:
                        loadMT(tt + 1)
                    for k4 in range(4):
                        i = tt * 4 + k4
                        b = i % 2
                        if i + 1 < NT128:
                            loadXI(i + 1)
                        for dt_ in range(4):
                            pb_ = pk % 4
                            pk += 1
                            for kc in range(16):
                                P.op("pe", MM(pq[pb_][:, :], mt[mb][:, kc, k4 * 128:(k4 + 1) * 128], wo[:, kc, dt_ * 512:(dt_ + 1) * 512],
                                              kc == 0, kc == 15),
                                     reads=["Gmt%d" % mb, "wo%d" % (kc // 4)], writes=["Gp%d" % pb_], signal=(kc == 15))
                            P.op("dve", TT(xo[b][:, dt_ * 512:(dt_ + 1) * 512], pq[pb_][:, :], xi[b][:, dt_ * 512:(dt_ + 1) * 512], ALU.add),
                                 reads=["Gp%d" % pb_, "Gxi%d" % b], writes=["Gxo%d" % b])
                        P.dma("sync", xm[i * 128:(i + 1) * 128, :], xo[b][:], reads=["Gxo%d" % b], key="Gxo%d" % b)
                P.flush()

        def phase_G(l, xdst, wg, wp, last=False):
            with ExitStack() as es:
                gbc = sb(es, nc, "H_g", [128, D], F32)
                if last:
                    fgb = sb(es, nc, "H_fg", [128, D], F32)
                    yo = [sb(es, nc, f"H_yo{i}", [128, D], F32) for i in range(2)]
                    fss = sb(es, nc, "H_fss", [128, NT128], F32)
                    frs = sb(es, nc, "H_frs", [128, NT128], F32)
                    P.dma("sync", fgb[:], fin_g.partition_broadcast(128), writes=["Hfg"], key="Hfg")
                xin = [sb(es, nc, f"H_x{i}", [128, D], F32) for i in range(4)]
                pin = [sb(es, nc, f"H_p{i}", [128, PLE], F32) for i in range(4)]
                pbf = [sb(es, nc, f"H_pb{i}", [128, PLE], BF16) for i in range(2)]
                hb = [sb(es, nc, f"H_hb{i}", [128, D], BF16) for i in range(2)]
                jk = sb(es, nc, "H_jk", [128, D], BF16)
                ssum = sb(es, nc, "H_ss", [128, NT128], F32)
                sq = sb(es, nc, "H_sq", [128, NT128], F32)
                rsd = sb(es, nc, "H_rs", [128, NT128], F32)
                hT2 = [sb(es, nc, f"H_hT{i}", [128, 16, 128], BF16) for i in range(2)]
                pT2 = [sb(es, nc, f"H_pT{i}", [128, 2, 128], BF16) for i in range(2)]
                sg2 = [sb(es, nc, f"H_sg{i}", [128, 512], F32) for i in range(2)]
                tq2 = [sb(es, nc, f"H_tq{i}", [128, 512], F32) for i in range(2)]
                xo = [sb(es, nc, f"H_xo{i}", [128, D], F32) for i in range(2)]
                ptr = [ps(es, nc, f"H_pt{i}", [128, 1024], BF16) for i in range(2)]
                ptp = ps(es, nc, "H_ptp", [128, 1024], BF16)
                pg = [ps(es, nc, f"H_pg{i}", [128, 512], F32) for i in range(2)]
                ppp = [ps(es, nc, f"H_pp{i}", [128, 512], F32) for i in range(2)]
                P.dma("sync", gbc[:], ple_g[l].partition_broadcast(128), writes=["Hg"], key="Hg")
                pk = [0]

                def loadH(i_):
                    b_ = i_ % 4
                    P.dma("sync", xin[b_][:], xm[i_ * 128:(i_ + 1) * 128, :], writes=["Hx%d" % b_], key="Hx%d" % b_)
                    P.dma("sync", pin[b_][:], p[l, i_ * 128:(i_ + 1) * 128, :], writes=["Hp%d" % b_], key="Hp%d" % b_)

                def normH(i):
                    b = i % 2
                    x3 = i % 4
                    P.op("pool", CP(pbf[b][:], pin[x3][:]), reads=["Hp%d" % x3], writes=["Hpbf%d" % b])
                    norm_tile_eps(P, "H", xin[x3], gbc, hb[b], ssum, sq, rsd, jk, i, xkey="Hx%d" % x3)

                def prepH(i):
                    b = i % 2
                    x3 = i % 4
                    hT = hT2[b]
                    pT = pT2[b]
                    for half in range(2):
                        for j in range(8):
                            kc = half * 8 + j
                            P.op("pe", TR(ptr[half][:, j * 128:(j + 1) * 128], hb[b][:, kc * 128:(kc + 1) * 128], ident[:]),
                                 reads=["Hhb%d" % b, "ident"], writes=["Hptr%d" % half], signal=(j == 7))
                        P.op("act", ACT(hT[:, half * 8:(half + 1) * 8, :], ptr[half][:].rearrange("p (j t) -> p j t", t=128), AF.Copy),
                             reads=["Hptr%d" % half], writes=["HhT%d" % b])
                    for j in range(2):
                        P.op("pe", TR(ptp[:, j * 128:(j + 1) * 128], pbf[b][:, j * 128:(j + 1) * 128], ident[:]),
                             reads=["Hpbf%d" % b, "ident"], writes=["Hptp"], signal=(j == 1))
                    P.op("act", ACT(pT[:], ptp[:, 0:256].rearrange("p (j t) -> p j t", t=128), AF.Copy),
                         reads=["Hptp"], writes=["HpT%d" % b])

                def mainH(i):
                    b = i % 2
                    x3 = i % 4
                    hT = hT2[b]
                    pT = pT2[b]
                    for dt_ in range(4):
                        ds_ = slice(dt_ * 512, (dt_ + 1) * 512)
                        k = pk[0] % 2
                        pk[0] += 1
                        for kc in range(16):
                            P.op("pe", MM(pg[k][:, :], hT[:, kc, :], wg[:, kc, ds_], kc == 0, kc == 15),
                                 reads=["HhT%d" % b, "wg"], writes=["Hpg%d" % k], signal=(kc == 15))
                        for kc in range(2):
                            P.op("pe", MM(ppp[k][:, :], pT[:, kc, :], wp[:, kc, ds_], kc == 0, kc == 1),
                                 reads=["HpT%d" % b, "wp"], writes=["Hpp%d" % k], signal=(kc == 1))
                        sg = sg2[k]
                        tq = tq2[k]
                        P.op("act", ACT(sg[:], pg[k][:, :], AF.Sigmoid), reads=["Hpg%d" % k], writes=["Hsg%d" % k])
                        P.op("dve", TT(tq[:], ppp[k][:, :], sg[:], ALU.mult), reads=["Hpp%d" % k, "Hsg%d" % k], writes=["Htq%d" % k])
                        P.op("pool", TT(xo[b][:, ds_], tq[:], xin[x3][:, ds_], ALU.add), reads=["Htq%d" % k, "Hx%d" % x3], writes=["Hxo%d" % b])
                    if not last:
                        P.dma("sync", xdst[i * 128:(i + 1) * 128, :], xo[b][:], reads=["Hxo%d" % b], key="Hxo%d" % b)
                    else:
                        P.op("dve", STT(jk[:], xo[b][:], 1.0, xo[b][:], ALU.mult, ALU.mult, accum_out=fss[:, i:i + 1]),
                             reads=["Hxo%d" % b], writes=["Hjk", "Hfss%d" % i])
                        P.op("dve", TS(fss[:, i:i + 1], fss[:, i:i + 1], 1.0 / D, EPS, ALU.mult, ALU.add),
                             reads=["Hfss%d" % i], writes=["Hfss%d" % i])
                        P.op("pool", TT(frs[:, i:i + 1], fss[:, i:i + 1], mhalf[:, 0:1], ALU.pow),
                             reads=["Hfss%d" % i, "mhalf"], writes=["Hfrs%d" % i])
                        P.op("dve", STT(yo[b][:], xo[b][:], frs[:, i:i + 1], fgb[:], ALU.mult, ALU.mult),
                             reads=["Hxo%d" % b, "Hfrs%d" % i, "Hfg"], writes=["Hyo%d" % b])
                        P.dma("sync", y[i * 128:(i + 1) * 128, :], yo[b][:], reads=["Hyo%d" % b], key="Hyo%d" % b)

                hb3 = hb
                for i0 in range(min(3, NT128)):
                    loadH(i0)
                normH(0)
                if NT128 > 1:
                    normH(1)
                prepH(0)
                for i in range(NT128):
                    if i + 3 < NT128:
                        loadH(i + 3)
                    if i + 1 < NT128:
                        prepH(i + 1)
                    if i + 2 < NT128:
                        normH(i + 2)
                    mainH(i)
                P.flush()

        def phase_final(xsrc):
            with ExitStack() as es:
                xin = [sb(es, nc, f"Z_x{i}", [128, D], F32) for i in range(2)]
                yo = [sb(es, nc, f"Z_y{i}", [128, D], F32) for i in range(2)]
                jk = sb(es, nc, "Z_jk", [128, D], BF16)
                gbc = sb(es, nc, "Z_g", [128, D], F32)
                ssum = sb(es, nc, "Z_ss", [128, NT128], F32)
                sq = sb(es, nc, "Z_sq", [128, NT128], F32)
                rsd = sb(es, nc, "Z_rs", [128, NT128], F32)
                P.dma("sync", gbc[:], fin_g.partition_broadcast(128), writes=["Zg"], key="Zg")
                def loadZ(i_):
                    b_ = i_ % 2
                    P.dma("sync", xin[b_][:], xsrc[i_ * 128:(i_ + 1) * 128, :], writes=["Zx%d" % b_], key="Zx%d" % b_)
                loadZ(0)
                for i in range(NT128):
                    b = i % 2
                    if i + 1 < NT128:
                        loadZ(i + 1)
                    P.op("dve", STT(jk[:], xin[b][:], 1.0, xin[b][:], ALU.mult, ALU.mult, accum_out=ssum[:, i:i + 1]),
                         reads=["Zx%d" % b], writes=["Zjk", "Zss%d" % i])
                    P.op("dve", TS(ssum[:, i:i + 1], ssum[:, i:i + 1], 1.0 / D, EPS, ALU.mult, ALU.add),
                         reads=["Zss%d" % i], writes=["Zss%d" % i])
                    P.op("pool", TT(rsd[:, i:i + 1], ssum[:, i:i + 1], mhalf[:, 0:1], ALU.pow),
                         reads=["Zss%d" % i, "mhalf"], writes=["Zrs%d" % i])
                    P.op("dve", STT(yo[b][:], xin[b][:], rsd[:, i:i + 1], gbc[:], ALU.mult, ALU.mult),
                         reads=["Zx%d" % b, "Zrs%d" % i, "Zg"], writes=["Zy%d" % b])
                    P.dma("sync", y[i * 128:(i + 1) * 128, :], yo[b][:], reads=["Zy%d" % b], key="Zy%d" % b)
                P.flush()

        xcur = x
        for l in range(NL):
            with ExitStack() as esb:
                hT = sb(esb, nc, "hT", [128, 16, T], BF16)
                phase_A(l, xcur, hT)
                phase_B(l, hT)
            phase_C(l)
            phase_D(l)
            phase_E(l)
            phase_F1(l)
            with ExitStack() as esw:
                wg_ = sb(esw, nc, "H_wg", [128, 16, D], BF16)
                wp_ = sb(esw, nc, "H_wp", [128, 2, D], BF16)
                phase_F2(l, xcur, wg_, wp_)
                phase_G(l, xs[l], wg_, wp_, last=(l == NL - 1))
            xcur = xs[l]
    return nc


def _bucket(dist):
    n = np.maximum(dist, 0)
    max_exact = 16
    large = max_exact + (np.log(np.maximum(n, 1).astype(np.float32) / max_exact)
                         / math.log(128 / max_exact) * (32 - max_exact)).astype(np.int32)
    large = np.minimum(large, 31)
    return np.where(n < max_exact, n, large)


def host_constants(T, rpe_table):
    NT128 = T // 128
    c = {}
    c["identf"] = np.eye(128, dtype=np.float32)
    em = np.zeros((16, 16, 128), np.float32)
    for n in range(16):
        em[n, n, :] = 30000.0
    c["emat"] = em.reshape(16, 16 * 128)
    pm = np.full((128, NT128, 16), -BIG, np.float32)
    for i in range(NT128):
        qb = i // 2
        pm[:, i, :qb] = 0.0
        pm[:, i, qb] = BIG
    c["pastm"] = pm.reshape(128, NT128 * 16)
    ki = np.arange(128)[:, None, None]
    jj = np.arange(6)[None, :, None]
    qi = np.arange(512)[None, None, :]
    dist = qi + 256 - jj * 128 - ki
    bk = _bucket(dist)
    tabA = rpe_table[:, :8]
    c["biasA"] = np.ascontiguousarray(np.transpose(tabA[bk], (3, 0, 1, 2))).astype(np.float32)
    c["maskA"] = np.where(dist >= 0, 0.0, NEGM).astype(np.float32)
    c["c31"] = np.ascontiguousarray(np.broadcast_to(tabA[31][None, :], (128, 8))).astype(np.float32)
    ki = np.arange(128)[:, None]
    qi = np.arange(128)[None, :]
    d_prev = qi + 128 - ki
    d_own = qi - ki
    tabB = rpe_table[:, 8:]
    bB = np.stack([np.transpose(tabB[_bucket(d_prev)], (0, 2, 1)),
                   np.transpose(tabB[_bucket(d_own)], (0, 2, 1))], axis=1)
    c["biasB"] = np.ascontiguousarray(bB).astype(np.float32)
    mB = np.stack([np.where((d_prev >= 0) & (d_prev < 128), 0.0, NEGM),
                   np.where((d_own >= 0) & (d_own < 128), 0.0, NEGM)], axis=1)
    c["maskB"] = np.ascontiguousarray(np.broadcast_to(mB[:, :, None, :], (128, 2, 16, 128))).astype(np.float32)
    return c


def host_cvec(conv_w, conv_b, b_r, b_i, lam):
    NL = conv_w.shape[0]
    f = np.concatenate([conv_w, conv_b[:, None], b_r[:, None], b_i[:, None], lam[:, None]], axis=1)
    f = f.reshape(NL, 8, 8, 128)
    return np.ascontiguousarray(np.transpose(f, (0, 3, 2, 1))).astype(np.float32)


_NC_CACHE = {}


def kernel(x, p, rpe_table, norm_g, w_in, sinks, conv_w, conv_b, w_r, b_r, w_i, b_i, lam,
           w_br, w_out, ple_norm_g, w_pg, w_pp, final_norm_g):
    x = np.asarray(x, np.float32)
    B, T, _ = x.shape
    NL = w_in.shape[0]
    key = (T, NL)
    if key not in _NC_CACHE:
        _NC_CACHE[key] = build_program(T, NL)
    nc = _NC_CACHE[key]
    consts = host_constants(T, np.asarray(rpe_table, np.float32))
    shared = dict(
        norm_g=np.asarray(norm_g, np.float32), w_in=np.asarray(w_in, np.float32),
        sinks=np.asarray(sinks, np.float32),
        cvec=host_cvec(np.asarray(conv_w, np.float32), np.asarray(conv_b, np.float32), np.asarray(b_r, np.float32),
                       np.asarray(b_i, np.float32), np.asarray(lam, np.float32)),
        w_r=np.asarray(w_r, np.float32), w_i=np.asarray(w_i, np.float32), w_br=np.asarray(w_br, np.float32),
        w_out=np.asarray(w_out, np.float32), ple_norm_g=np.asarray(ple_norm_g, np.float32),
        w_pg=np.asarray(w_pg, np.float32), w_pp=np.asarray(w_pp, np.float32),
        final_norm_g=np.asarray(final_norm_g, np.float32), **consts)
    n_cores = 8
    active = [0, 1, 4, 5][:B]
    parr = np.asarray(p, np.float32)
    zeros = {k: (v if k in consts else np.zeros_like(v)) for k, v in shared.items()}
    zx = np.zeros_like(x[0])
    zp = np.zeros_like(parr[:, 0])
    in_maps = []
    for c in range(n_cores):
        if c in active:
            b = active.index(c)
            m = dict(shared)
            m["x"] = np.ascontiguousarray(x[b])
            m["p"] = np.ascontiguousarray(parr[:, b])
        else:
            m = dict(zeros)
            m["x"] = zx
            m["p"] = zp
        in_maps.append(m)
    res = run_bass_kernel_spmd(nc, in_maps, core_ids=list(range(n_cores)))
    return np.stack([np.asarray(res.results[c]["y"], np.float32) for c in active], axis=0)
```
